# Optimizing a Trainium2 kernel written in Bass

```python
import math
import jax, jax.numpy as jnp
from jax import lax
import numpy as np

D_MODEL = 1024
BATCH = 2
SEQ = 8192
DEPTH = 1
DEC_BATCH = 8
DEC_SEQ = 8192
PAST_LEN = 128

MIX_WIDTH = D_MODEL
ATTN_WIDTH = MIX_WIDTH // 2
REC_WIDTH = MIX_WIDTH - ATTN_WIDTH
ATTN_HEADS = 4
ATTN_VDIM = ATTN_WIDTH // ATTN_HEADS
ATTN_QKDIM = ATTN_VDIM // 2
REC_HEADS = 4
REC_KDIM = REC_WIDTH // REC_HEADS
REC_VDIM = REC_WIDTH // REC_HEADS
D_FF = ((8 * D_MODEL // 3 + 127) // 128) * 128
CONV_WIDTH = 3
Q_BLOCK = 128
CHUNK = 64
NORM_EPS = 1e-6
IN_WIDTH = 3 * ATTN_WIDTH + 5 * REC_WIDTH

kernel_name = "hymba_diffattn_hgrn2_convffn_encoder"


def rms_norm(x, w, eps=NORM_EPS):
    x32 = x.astype(jnp.float32)
    y = x32 * lax.rsqrt(jnp.mean(x32 * x32, axis=-1, keepdims=True) + eps)
    return (y * w.astype(jnp.float32)).astype(x.dtype)


def alibi_slopes(n_heads):
    start = 2.0 ** (-8.0 / n_heads)
    return jnp.asarray(np.array([start ** (i + 1) for i in range(n_heads)], dtype=np.float32))


def diff_attention(q, k, v, lam, slopes):
    B, H, _, L, dk = q.shape
    nb = L // Q_BLOCK
    qb = q.reshape(B, H, 2, nb, Q_BLOCK, dk).transpose(3, 0, 1, 2, 4, 5)
    qpos = jnp.arange(L, dtype=jnp.int32).reshape(nb, Q_BLOCK)
    kpos = jnp.arange(L, dtype=jnp.int32)
    scale = dk ** -0.5

    def block(args):
        qblk, pos = args
        s = jnp.einsum('bhjqd,bhjkd->bhjqk', qblk, k) * scale
        dist = jnp.abs(pos[:, None] - kpos[None, :]).astype(jnp.float32)
        s = s - slopes[None, :, None, None, None] * dist[None, None, None]
        p = jax.nn.softmax(s, axis=-1)
        a = p[:, :, 0] - lam * p[:, :, 1]
        return jnp.einsum('bhqk,bhkv->bhqv', a, v)

    o = lax.map(block, (qb, qpos))
    return o.transpose(1, 0, 3, 2, 4).reshape(B, L, H, -1)


def gla_chunk_scan(q, k, v, logf):
    B, H, L, K = q.shape
    V = v.shape[-1]
    n = L // CHUNK

    def to_chunks(t):
        return t.reshape(B, H, n, CHUNK, t.shape[-1]).transpose(2, 0, 1, 3, 4)

    mask = jnp.tril(jnp.ones((CHUNK, CHUNK), dtype=bool))

    def step(S, xs):
        qc, kc, vc, lfc = xs
        G = jnp.cumsum(lfc, axis=-2)
        o_inter = jnp.einsum('bhtk,bhkv->bhtv', qc * jnp.exp(G), S)
        diff = G[:, :, :, None, :] - G[:, :, None, :, :]
        decay = jnp.exp(jnp.where(mask[:, :, None], diff, -jnp.inf))
        A = jnp.einsum('bhtk,bhsk,bhtsk->bhts', qc, kc, decay)
        o_intra = jnp.einsum('bhts,bhsv->bhtv', A, vc)
        G_last = G[:, :, -1:, :]
        S_new = (jnp.exp(G_last[:, :, 0, :])[..., None] * S
                 + jnp.einsum('bhsk,bhsv->bhkv', kc * jnp.exp(G_last - G), vc))
        return S_new, o_inter + o_intra

    S0 = jnp.zeros((B, H, K, V), jnp.float32)
    _, o = lax.scan(step, S0, (to_chunks(q), to_chunks(k), to_chunks(v), to_chunks(logf)))
    return o.transpose(1, 2, 0, 3, 4).reshape(B, H, L, V)


def encoder_layer(x, l, norm_mix_w, w_in, q_norm_w, k_norm_w, lambda_q1, lambda_k1,
                  lambda_q2, lambda_k2, attn_out_norm_w, lb_fwd, lb_bwd, rec_out_norm_w,
                  w_out, norm_ffn_w, w_up, conv_w, conv_b, w_down):
    f32 = jnp.float32
    B, L, _ = x.shape
    h = rms_norm(x, norm_mix_w)
    proj = h @ w_in
    sizes = [ATTN_WIDTH] * 3 + [REC_WIDTH] * 5
    idx = np.cumsum(sizes)[:-1].tolist()
    aq, ak, av, rq, rf_f, rf_b, ri, rg = jnp.split(proj, idx, axis=-1)

    aq = rms_norm(aq.reshape(B, L, ATTN_HEADS, 2, ATTN_QKDIM), q_norm_w)
    ak = rms_norm(ak.reshape(B, L, ATTN_HEADS, 2, ATTN_QKDIM), k_norm_w)
    aq = aq.transpose(0, 2, 3, 1, 4).astype(f32)
    ak = ak.transpose(0, 2, 3, 1, 4).astype(f32)
    av = av.reshape(B, L, ATTN_HEADS, ATTN_VDIM).transpose(0, 2, 1, 3).astype(f32)
    lam_init = 0.8 - 0.6 * math.exp(-0.3 * l)
    lam = (jnp.exp(jnp.sum(lambda_q1.astype(f32) * lambda_k1.astype(f32)))
           - jnp.exp(jnp.sum(lambda_q2.astype(f32) * lambda_k2.astype(f32))) + lam_init)
    ao = diff_attention(aq, ak, av, lam, alibi_slopes(ATTN_HEADS))
    ao = rms_norm(ao, attn_out_norm_w) * (1.0 - lam_init)
    ao = ao.reshape(B, L, ATTN_WIDTH).astype(x.dtype)

    def heads(t):
        return t.reshape(B, L, REC_HEADS, -1).transpose(0, 2, 1, 3).astype(f32)

    rq_h = heads(jax.nn.silu(rq))
    ri_h = heads(ri)

    def direction(f_logits, lb_table, reverse):
        lb = jnp.cumsum(jax.nn.softmax(lb_table.astype(f32), axis=0), axis=0)[l]
        f = heads(lb + (1.0 - lb) * jax.nn.sigmoid(f_logits.astype(f32)))
        logf = jnp.log(f)
        kk = 1.0 - f
        qq, vv = rq_h, ri_h
        if reverse:
            qq, kk, vv, logf = (jnp.flip(t, axis=2) for t in (qq, kk, vv, logf))
        o = gla_chunk_scan(qq, kk, vv, logf)
        return jnp.flip(o, axis=2) if reverse else o

    ro = direction(rf_f, lb_fwd, False) + direction(rf_b, lb_bwd, True)
    ro = ro.transpose(0, 2, 1, 3)
    ro = rms_norm(ro, rec_out_norm_w) * jax.nn.silu(
        rg.reshape(B, L, REC_HEADS, REC_VDIM).astype(f32))
    ro = ro.reshape(B, L, REC_WIDTH).astype(x.dtype)

    x = x + jnp.concatenate([ao, ro], axis=-1) @ w_out

    h = rms_norm(x, norm_ffn_w)
    u = h @ w_up
    pad = CONV_WIDTH // 2
    u_pad = jnp.pad(u, ((0, 0), (pad, pad), (0, 0)))
    c = conv_b
    for j in range(CONV_WIDTH):
        c = c + u_pad[:, j:j + L] * conv_w[j]
    gate, up = jnp.split(c, 2, axis=-1)
    x = x + (jax.nn.silu(gate) * up) @ w_down
    return x


def setup_inputs(seed: int = 0) -> dict:
    key = jax.random.key(seed)
    ks = jax.random.split(key, 24)
    f32 = jnp.float32

    def nrm(k, shape, scale):
        return jax.random.normal(k, shape, f32) * scale

    return {
        "x_prompt": nrm(ks[0], (BATCH, SEQ, D_MODEL), 1.0),
        "x_sample": nrm(ks[1], (DEC_BATCH, DEC_SEQ, D_MODEL), 1.0),
        "norm_mix_w": 1.0 + nrm(ks[2], (DEPTH, D_MODEL), 0.02),
        "w_in": nrm(ks[3], (DEPTH, D_MODEL, IN_WIDTH), D_MODEL ** -0.5),
        "q_norm_w": 1.0 + nrm(ks[4], (DEPTH, ATTN_QKDIM), 0.02),
        "k_norm_w": 1.0 + nrm(ks[5], (DEPTH, ATTN_QKDIM), 0.02),
        "lambda_q1": nrm(ks[6], (DEPTH, ATTN_QKDIM), 0.1),
        "lambda_k1": nrm(ks[7], (DEPTH, ATTN_QKDIM), 0.1),
        "lambda_q2": nrm(ks[8], (DEPTH, ATTN_QKDIM), 0.1),
        "lambda_k2": nrm(ks[9], (DEPTH, ATTN_QKDIM), 0.1),
        "attn_out_norm_w": 1.0 + nrm(ks[10], (DEPTH, ATTN_VDIM), 0.02),
        "lb_fwd": nrm(ks[11], (DEPTH + 1, REC_WIDTH), 0.5),
        "lb_bwd": nrm(ks[12], (DEPTH + 1, REC_WIDTH), 0.5),
        "rec_out_norm_w": 1.0 + nrm(ks[13], (DEPTH, REC_VDIM), 0.02),
        "w_out": nrm(ks[14], (DEPTH, MIX_WIDTH, D_MODEL), MIX_WIDTH ** -0.5),
        "norm_ffn_w": 1.0 + nrm(ks[15], (DEPTH, D_MODEL), 0.02),
        "w_up": nrm(ks[16], (DEPTH, D_MODEL, 2 * D_FF), D_MODEL ** -0.5),
        "conv_w": nrm(ks[17], (DEPTH, CONV_WIDTH, 2 * D_FF), CONV_WIDTH ** -0.5),
        "conv_b": nrm(ks[18], (DEPTH, 2 * D_FF), 0.02),
        "w_down": nrm(ks[19], (DEPTH, D_FF, D_MODEL), D_FF ** -0.5),
    }


def run_trunk(x, norm_mix_w, w_in, q_norm_w, k_norm_w, lambda_q1, lambda_k1, lambda_q2,
              lambda_k2, attn_out_norm_w, lb_fwd, lb_bwd, rec_out_norm_w, w_out,
              norm_ffn_w, w_up, conv_w, conv_b, w_down):
    for l in range(DEPTH):
        x = encoder_layer(x, l, norm_mix_w[l], w_in[l], q_norm_w[l], k_norm_w[l],
                          lambda_q1[l], lambda_k1[l], lambda_q2[l], lambda_k2[l],
                          attn_out_norm_w[l], lb_fwd, lb_bwd, rec_out_norm_w[l], w_out[l],
                          norm_ffn_w[l], w_up[l], conv_w[l], conv_b[l], w_down[l])
    return x


def reference(x_prompt, x_sample, norm_mix_w, w_in, q_norm_w, k_norm_w, lambda_q1,
              lambda_k1, lambda_q2, lambda_k2, attn_out_norm_w, lb_fwd, lb_bwd,
              rec_out_norm_w, w_out, norm_ffn_w, w_up, conv_w, conv_b, w_down):
    y_prompt = run_trunk(x_prompt, norm_mix_w, w_in, q_norm_w, k_norm_w, lambda_q1,
                         lambda_k1, lambda_q2, lambda_k2, attn_out_norm_w, lb_fwd, lb_bwd,
                         rec_out_norm_w, w_out, norm_ffn_w, w_up, conv_w, conv_b, w_down)
    y_sample = run_trunk(x_sample, norm_mix_w, w_in, q_norm_w, k_norm_w, lambda_q1,
                         lambda_k1, lambda_q2, lambda_k2, attn_out_norm_w, lb_fwd, lb_bwd,
                         rec_out_norm_w, w_out, norm_ffn_w, w_up, conv_w, conv_b, w_down)
    return (y_prompt, y_sample)
```

```python
import os
import numpy as np
import ml_dtypes
import concourse.bass as bass
import concourse.mybir as mybir
from concourse.bass_utils import run_bass_kernel_spmd

F32 = mybir.dt.float32
BF16 = mybir.dt.bfloat16
AF = mybir.ActivationFunctionType
ALU = mybir.AluOpType

D = 1024
DFF = 2816
NH = 4
EPS = 1e-6
SLOPES = [2.0 ** (-2 * (i + 1)) for i in range(NH)]
LAM_INIT = 0.8 - 0.6 * 1.0
SKIP_T = 60.0
P4STAGE = int(os.environ.get('P4STAGE', '9'))
P4SUB = int(os.environ.get('P4SUB', '9'))
SKIP123 = int(os.environ.get('SKIP123', '0'))
FW = 256


class Sched:
    EPOCH = int(os.environ.get("EPOCH", "12000"))

    def __init__(self, nc, ndma=10):
        self.nc = nc
        self.eng = {"pe": nc.tensor, "act": nc.scalar, "dve": nc.vector, "pool": nc.gpsimd, "sp": nc.sync}
        self.sem = {}
        self.cnt = {}
        self.cur = {}
        self.nsem = 0
        self.waited = {k: {} for k in self.eng}
        for k in ["pe", "act", "dve", "pool"]:
            self._new_epoch(k)
        self.dring = {}
        for q in ["sp", "pool"]:
            self.dring[q] = [[self._new_dsem(q) for _ in range(ndma)], 0]
        self.lastw = {}
        self.lastr = {}
        self.pending = {k: [] for k in self.eng}

    def _alloc(self, name):
        cm = self.nc.semaphore(name)
        s = cm.__enter__()
        self.nsem += 1
        return s

    def _new_epoch(self, e):
        key = f"{e}#{self.nsem}"
        self.sem[key] = self._alloc("s_" + key.replace("#", "_"))
        self.cur[e] = key
        self.cnt[key] = 0

    def _new_dsem(self, q):
        key = f"d{q}#{self.nsem}"
        self.sem[key] = self._alloc("s_" + key.replace("#", "_"))
        return [key, 0]

    def _wait(self, e, semkey, val):
        if val <= 0:
            return
        w = self.waited[e]
        if w.get(semkey, 0) >= val:
            return
        self.eng[e].wait_ge(self.sem[semkey], val)
        w[semkey] = val

    def _deps(self, e, reads, writes):
        skip = self.cur[e].split("#")[0] if e == "pe" else None
        def need(tok):
            if skip is not None and tok[0].split("#")[0] == skip:
                return
            self._wait(e, tok[0], tok[1])
        for b in reads:
            if b in self.lastw:
                need(self.lastw[b])
        for b in writes:
            if b in self.lastw:
                need(self.lastw[b])
            for r in self.lastr.get(b, ()):
                need(r)

    def _commit(self, tok, reads, writes):
        for b in writes:
            self.lastw[b] = tok
            self.lastr[b] = []
        for b in reads:
            lst = self.lastr.setdefault(b, [])
            lst[:] = [t for t in lst if t[0] != tok[0]]
            lst.append(tok)

    def op(self, e, fn, reads=(), writes=(), inc=True):
        self._deps(e, reads, writes)
        ins = fn(self.eng[e])
        key = self.cur[e]
        if inc:
            self.cnt[key] += 1
            ins.then_inc(self.sem[key], 1)
            tok = (key, self.cnt[key])
            self._commit(tok, reads, writes)
            if self.cnt[key] >= self.EPOCH:
                self._new_epoch(e)
        else:
            tok = (key, self.cnt[key] + 1)
            self._commit(tok, reads, writes)
        return ins

    def pe(self, fn, r=(), w=(), inc=True):
        return self.op("pe", fn, r, w, inc)

    def act(self, fn, r=(), w=()):
        return self.op("act", fn, r, w)

    def dve(self, fn, r=(), w=()):
        return self.op("dve", fn, r, w)

    def pool(self, fn, r=(), w=()):
        return self.op("pool", fn, r, w)

    def dma(self, q, out, in_, reads=(), writes=()):
        ring, pos = self.dring[q]
        slot = ring[pos]
        if slot[1] >= int(os.environ.get("DROLL", "900")):
            self._wait(q, slot[0], 16 * slot[1])
            ring[pos] = slot = self._new_dsem(q)
        self.dring[q][1] = (pos + 1) % len(ring)
        key, n = slot
        self._wait(q, key, 16 * n)
        self._deps(q, reads, writes)
        self.eng[q].dma_start(out=out, in_=in_).then_inc(self.sem[key], 16)
        slot[1] = n + 1
        self._commit((key, 16 * (n + 1)), reads, writes)

    def ld(self, out, in_, r=(), w=()):
        self.dma("sp", out, in_, r, w)

    def st(self, out, in_, r=(), w=()):
        self.dma("pool", out, in_, r, w)

    def barrier(self):
        toks = []
        for e in ["pe", "act", "dve", "pool"]:
            key = self.cur[e]
            if self.cnt[key] > 0:
                toks.append((key, self.cnt[key]))
        for q in self.dring:
            for key, n in self.dring[q][0]:
                if n > 0:
                    toks.append((key, 16 * n))
        for e in self.eng:
            for t in toks:
                self._wait(e, *t)
        self.lastw.clear()
        self.lastr.clear()

    def flush_pe(self, dummy_fn):
        pass


def host_consts(L):
    bf = ml_dtypes.bfloat16
    c = {}
    c["c_identb"] = np.eye(128, dtype=np.float32).astype(bf)
    c["c_identf"] = np.eye(128, dtype=np.float32)
    c["c_onesb"] = np.ones((128, 128), np.float32).astype(bf)
    blk = np.zeros((128, 128), np.float32)
    blk[:64, :64] = 1.0
    blk[64:, 64:] = 1.0
    c["c_blkb"] = blk.astype(bf)
    s = np.arange(128)[:, None]
    t = np.arange(128)[None, :]
    c["c_maskf"] = (s <= t).astype(np.float32)
    c["c_maskb"] = (s >= t).astype(np.float32)
    c["c_onesf"] = np.ones((128, 128), np.float32)
    k = np.arange(L)
    kaug = np.stack([np.ones(L), np.ones(L), np.ones(L), k % 128, (k // 128) * 128]).astype(np.float32)
    c["c_kaug"] = kaug.astype(bf)
    q = np.arange(L)
    r = (q % 512) % 256
    b256 = ((q % 512) // 256) * 256
    q0 = (q // 512) * 512
    qa = np.zeros((NH, 2, 5, L), np.float32)
    for h in range(NH):
        sl = SLOPES[h]
        left = np.stack([-sl * r, -sl * b256, -sl * q0, sl * np.ones(L), sl * np.ones(L)])
        qa[h, 0] = left
        qa[h, 1] = -left
    c["c_qaug"] = qa.astype(bf)
    assert np.array_equal(c["c_qaug"].astype(np.float32), qa)
    assert np.array_equal(c["c_kaug"].astype(np.float32), kaug)
    p = np.arange(128)[:, None]
    x = np.arange(512)[None, :]
    c["c_bbase"] = np.stack([-np.abs(128 * i + p - x) for i in range(4)]).astype(np.float32)
    return c


def build(L, NSEQ, debug=False, upto=9):
    NB = L // 128
    NT = L // 512
    nc = bass.Bass("TRN2", target_bir_lowering=False)
    S = Sched(nc)

    def din(name, shape, dt=F32):
        return nc.dram_tensor(name, list(shape), dt, kind="ExternalInput").ap()

    def dscr(name, shape, dt):
        return nc.dram_tensor(name, list(shape), dt, kind="ExternalOutput" if debug else "Internal").ap()

    x = din("x", [NSEQ, L, D])
    y = nc.dram_tensor("y", [NSEQ, L, D], F32, kind="ExternalOutput").ap()
    w_in = din("w_in", [D, 4096])
    w_out = din("w_out", [D, D])
    w_up = din("w_up", [D, 2 * DFF])
    w_down = din("w_down", [DFF, D])
    p_nmw = din("p_nmw", [128, 8])
    p_nfw = din("p_nfw", [128, 8])
    p_qnw = din("p_qnw", [128, 1])
    p_knw = din("p_knw", [128, 1])
    p_lam = din("p_lam", [128, 4, 64])
    p_aonw = din("p_aonw", [128, 1])
    p_ronw = din("p_ronw", [128, 1])
    p_lb = din("p_lb", [128, 2, NH, 2])
    p_cw = din("p_cw", [128, 44, 3])
    p_cb = din("p_cb", [128, 44])
    c_identb = din("c_identb", [128, 128], BF16)
    c_identf = din("c_identf", [128, 128])
    c_onesb = din("c_onesb", [128, 128], BF16)
    c_blkb = din("c_blkb", [128, 128], BF16)
    c_maskf = din("c_maskf", [128, 128])
    c_maskb = din("c_maskb", [128, 128])
    c_onesf = din("c_onesf", [128, 128])
    c_kaug = din("c_kaug", [5, L], BF16)
    c_qaug = din("c_qaug", [NH, 2, 5, L], BF16)
    c_bbase = din("c_bbase", [4, 128, 512])

    WIN = dscr("WIN", [128, 8, 4096], BF16)
    WOUT = dscr("WOUT", [128, 8, D], BF16)
    WUP = dscr("WUP", [128, 8, 2 * DFF], BF16)
    WDN = dscr("WDN", [8, 128, 22, 128], BF16)
    QT = dscr("QT", [NSEQ, NH, 128, L], BF16)
    KT = dscr("KT", [NSEQ, NH, 128, L], BF16)
    VV = dscr("VV", [NSEQ, NH, 128, NB, 128], BF16)
    RQ = dscr("RQ", [NSEQ, NH, 128, L], BF16)
    LOGF = dscr("LOGF", [NSEQ, 2, NH, 128, L], F32)
    KK = dscr("KK", [NSEQ, 2, NH, 128, L], BF16)
    RI = dscr("RI", [NSEQ, NH, 128, NB, 128], BF16)
    RG = dscr("RG", [NSEQ, NH, 128, L], BF16)
    MIXT = dscr("MIXT", [NSEQ, 8, 128, L], BF16)

    held = []

    def sb(name, shape, dt):
        cm = nc.sbuf_tensor(name, list(shape), dt)
        t = cm.__enter__()
        held.append(cm)
        return t

    class Scope:
        def __init__(self):
            self.cms = []

        def sb(self, name, shape, dt):
            cm = nc.sbuf_tensor(name, list(shape), dt)
            t = cm.__enter__()
            self.cms.append(cm)
            return t

        def ps(self, name, shape, dt=F32):
            cm = nc.psum_tensor(name, list(shape), dt)
            t = cm.__enter__()
            self.cms.append(cm)
            return t

        def close(self):
            S.barrier()
            for cm in reversed(self.cms):
                cm.__exit__(None, None, None)
            self.cms = []

    identb = sb("identb", [128, 128], BF16)
    identf = sb("identf", [128, 128], F32)
    onesb = sb("onesb", [128, 128], BF16)
    blkb = sb("blkb", [128, 128], BF16)
    maskf = sb("maskf", [128, 128], F32)
    maskb = sb("maskb", [128, 128], F32)
    onesf = sb("onesf", [128, 128], F32)
    nmw = sb("nmw", [128, 8], F32)
    nfw = sb("nfw", [128, 8], F32)
    qnw = sb("qnw", [128, 1], F32)
    knw = sb("knw", [128, 1], F32)
    lamv = sb("lamv", [128, 4, 64], F32)
    aonw = sb("aonw", [128, 1], F32)
    ronw = sb("ronw", [128, 1], F32)
    lbt = sb("lbt", [128, 2, NH, 2], F32)
    cw = sb("cw", [128, 44, 3], F32)
    cb = sb("cb", [128, 44], F32)
    lbv = sb("lbv", [128, 2, NH], F32)
    olb = sb("olb", [128, 2, NH], F32)
    nlam = sb("nlam", [128, 1], F32)
    sm = sb("sm", [128, 8], F32)

    for t, src, key in [(identb, c_identb, "identb"), (identf, c_identf, "identf"), (onesb, c_onesb, "onesb"),
                        (blkb, c_blkb, "blkb"), (maskf, c_maskf, "maskf"), (maskb, c_maskb, "maskb"),
                        (onesf, c_onesf, "onesf"), (nmw, p_nmw, "nmw"), (nfw, p_nfw, "nfw"), (qnw, p_qnw, "qnw"),
                        (knw, p_knw, "knw"), (aonw, p_aonw, "aonw"), (ronw, p_ronw, "ronw"), (cb, p_cb, "cb")]:
        S.ld(t[:], src[:, :], w=[key])
    S.ld(lamv[:], p_lam[:, :, :], w=["lamv"])
    S.ld(lbt[:], p_lb[:, :, :, :], w=["lbt"])
    S.ld(cw[:], p_cw[:, :, :], w=["cw"])

    S.dve(lambda e: e.tensor_scalar(out=qnw[:], in0=qnw[:], scalar1=0.125, scalar2=None, op0=ALU.mult), ["qnw"], ["qnw"])
    S.dve(lambda e: e.tensor_scalar(out=aonw[:], in0=aonw[:], scalar1=1.0 - LAM_INIT, scalar2=None, op0=ALU.mult), ["aonw"], ["aonw"])
    prod = sb("prod", [128, 2, 64], F32)
    S.dve(lambda e: e.tensor_tensor(out=prod[:, 0, :], in0=lamv[:, 0, :], in1=lamv[:, 1, :], op=ALU.mult), ["lamv"], ["prod"])
    S.dve(lambda e: e.tensor_tensor(out=prod[:, 1, :], in0=lamv[:, 2, :], in1=lamv[:, 3, :], op=ALU.mult), ["lamv", "prod"], ["prod"])
    S.dve(lambda e: e.tensor_reduce(out=sm[:, 0:2], in_=prod[:], axis=mybir.AxisListType.X, op=ALU.add), ["prod"], ["sm"])
    S.act(lambda e: e.activation(out=sm[:, 2:4], in_=sm[:, 0:2], func=AF.Exp), ["sm"], ["sm"])
    S.dve(lambda e: e.scalar_tensor_tensor(out=nlam[:], in0=sm[:, 3:4], scalar=-LAM_INIT, in1=sm[:, 2:3],
                                           op0=ALU.add, op1=ALU.subtract), ["sm"], ["nlam"])
    lbtmp = sb("lbtmp", [128, 2, NH], F32)
    S.dve(lambda e: e.tensor_tensor(out=lbtmp[:], in0=lbt[:, :, :, 1], in1=lbt[:, :, :, 0], op=ALU.subtract), ["lbt"], ["lbtmp"])
    S.act(lambda e: e.activation(out=lbtmp[:], in_=lbtmp[:], func=AF.Exp), ["lbtmp"], ["lbtmp"])
    S.dve(lambda e: e.tensor_scalar(out=lbtmp[:], in0=lbtmp[:], scalar1=1.0, scalar2=None, op0=ALU.add), ["lbtmp"], ["lbtmp"])
    S.dve(lambda e: e.reciprocal(out=lbv[:], in_=lbtmp[:]), ["lbtmp"], ["lbv"])
    S.dve(lambda e: e.tensor_scalar(out=olb[:], in0=lbv[:], scalar1=-1.0, scalar2=1.0, op0=ALU.mult, op1=ALU.add), ["lbv"], ["olb"])

    def rstd_from(sc, ss_ap, n, out_ap, tmp_ap, rkeys, wkeys):
        S.act(lambda e: e.activation(out=tmp_ap, in_=ss_ap, func=AF.Ln, scale=1.0 / n, bias=epsb[:, 0:1]), rkeys, wkeys)
        S.act(lambda e: e.activation(out=out_ap, in_=tmp_ap, func=AF.Exp, scale=-0.5), wkeys, wkeys)

    epsb = sb("epsb", [128, 1], F32)
    S.dve(lambda e: e.memset(epsb[:], EPS), [], ["epsb"])
    S.barrier()

    sc = Scope()
    stg = [sc.sb(f"stg{i}", [128, 2048], F32) for i in range(2)]
    stb = [sc.sb(f"stb{i}", [128, 2048], BF16) for i in range(2)]
    it = 0

    def conv_piece(src_ap, width, scale_ap, dst_ap, dst_view=None):
        nonlocal it
        i = it % 2
        it += 1
        S.ld(stg[i][:, 0:width], src_ap, w=[f"stg{i}"])
        if scale_ap is not None:
            S.dve(lambda e: e.tensor_scalar(out=stb[i][:, 0:width], in0=stg[i][:, 0:width], scalar1=scale_ap,
                                            scalar2=None, op0=ALU.mult), [f"stg{i}"], [f"stb{i}"])
        else:
            S.dve(lambda e: e.tensor_copy(out=stb[i][:, 0:width], in_=stg[i][:, 0:width]), [f"stg{i}"], [f"stb{i}"])
        src = stb[i][:, 0:width] if dst_view is None else dst_view(stb[i])
        S.st(dst_ap, src, r=[f"stb{i}"])

    for c in range(8):
        for hf in range(2):
            conv_piece(w_in[c * 128:(c + 1) * 128, hf * 2048:(hf + 1) * 2048], 2048, nmw[:, c:c + 1],
                       WIN[:, c, hf * 2048:(hf + 1) * 2048])
        conv_piece(w_out[c * 128:(c + 1) * 128, :], 1024, None, WOUT[:, c, :])
        for (o, wd) in [(0, 2048), (2048, 2048), (4096, 1536)]:
            conv_piece(w_up[c * 128:(c + 1) * 128, o:o + wd], wd, nfw[:, c:c + 1], WUP[:, c, o:o + wd])
    for j in range(22):
        conv_piece(w_down[j * 128:(j + 1) * 128, :], 1024, None,
                   WDN[:, :, j, :].rearrange("c p n -> p c n"),
                   dst_view=lambda t: t[:, 0:1024].rearrange("p (c n) -> p c n", c=8))
    sc.close()

    if upto == 0:
        return nc
    sc = Scope()
    win = sc.sb("win", [128, 8, 4096], BF16)
    for c in range(8):
        S.ld(win[:, c, :], WIN[:, c, :], w=["win"])
    xt = [sc.sb(f"xt{i}", [128, 4, D], F32) for i in range(2)]
    junk = sc.sb("junk", [128, D], BF16)
    hb = [sc.sb(f"hb{i}", [128, D], BF16) for i in range(2)]
    hT = [sc.sb(f"hT{i}", [128, 8, 512], BF16) for i in range(2)]
    ssb = sc.sb("ssb", [128, 4], F32)
    sqb = [sc.sb(f"sqb{i}", [128, 512], BF16) for i in range(2)]
    t1 = [sc.sb(f"t1{i}", [128, 512], F32) for i in range(2)]
    rs = [sc.sb(f"rs{i}", [128, 512], F32) for i in range(2)]
    ob = [sc.sb(f"ob{i}", [128, 512], BF16) for i in range(4)]
    of32 = [sc.sb(f"of{i}", [128, 512], F32) for i in range(2)]
    lf = [sc.sb(f"lf{i}", [128, 512], F32) for i in range(2)]
    tp = sc.ps("tp", [128, 8, 128], BF16)
    pq = [sc.ps(f"pq{i}", [128, 512]) for i in range(4)]
    pss = [sc.ps(f"pss{i}", [128, 512]) for i in range(2)]
    obi = 0
    gi = 0
    for s in range(0 if SKIP123 else NSEQ):
        for tt in range(NT):
            t0 = tt * 512
            xi = (s * NT + tt) % 2
            X, XK = xt[xi], f"xt{xi}"
            HT, HK = hT[xi], f"hT{xi}"
            S.ld(X[:], x[s, t0:t0 + 512, :].rearrange("(b p) f -> p b f", p=128), w=[XK])
            for b in range(4):
                hbi = b % 2
                S.act(lambda e: e.activation(out=junk[:], in_=X[:, b, :], func=AF.Square, accum_out=ssb[:, b:b + 1]),
                      [XK], ["junk", "ssb"])
                rstd_from(sc, ssb[:, b:b + 1], D, ssb[:, b:b + 1], ssb[:, b:b + 1], ["ssb"], ["ssb"])
                S.dve(lambda e: e.tensor_scalar(out=hb[hbi][:], in0=X[:, b, :], scalar1=ssb[:, b:b + 1], scalar2=None,
                                                op0=ALU.mult), [XK, "ssb"], [f"hb{hbi}"])
                for c in range(8):
                    S.pe(lambda e: e.transpose(out=tp[:, c, :], in_=hb[hbi][:, c * 128:(c + 1) * 128], identity=identb[:]),
                         [f"hb{hbi}"], ["tp"], inc=(c == 7))
                S.dve(lambda e: e.tensor_copy(out=HT[:, :, b * 128:(b + 1) * 128], in_=tp[:]), ["tp"], [HK])

            def proj_fm(col0, pi):
                for c in range(8):
                    S.pe(lambda e: e.matmul(pq[pi][:], lhsT=win[:, c, col0:col0 + 128], rhs=HT[:, c, :],
                                            start=(c == 0), stop=(c == 7)), ["win", HK], [f"pq{pi}"], inc=(c == 7))

            def proj_tm(col0, b, pi):
                for c in range(8):
                    S.pe(lambda e: e.matmul(pq[pi][:], lhsT=HT[:, c, b * 128:(b + 1) * 128], rhs=win[:, c, col0:col0 + 512],
                                            start=(c == 0), stop=(c == 7)), ["win", HK], [f"pq{pi}"], inc=(c == 7))

            for g in range(8):
                isk = g >= 4
                h = g % 4
                pi = gi % 4
                si = gi % 2
                gi += 1
                proj_fm((512 if isk else 0) + h * 128, pi)
                S.act(lambda e: e.activation(out=sqb[si][:], in_=pq[pi][:], func=AF.Square), [f"pq{pi}"], [f"sqb{si}"])
                S.pe(lambda e: e.matmul(pss[si][:], lhsT=blkb[:], rhs=sqb[si][:], start=True, stop=True),
                     [f"sqb{si}", "blkb"], [f"pss{si}"])
                rstd_from(sc, pss[si][:], 64, rs[si][:], t1[si][:], [f"pss{si}"], [f"t1{si}", f"rs{si}"])
                oi = obi % 4
                obi += 1
                wcol = knw if isk else qnw
                S.dve(lambda e: e.scalar_tensor_tensor(out=ob[oi][:], in0=pq[pi][:], scalar=wcol[:, 0:1], in1=rs[si][:],
                                                       op0=ALU.mult, op1=ALU.mult),
                      [f"pq{pi}", f"rs{si}", "qnw", "knw"], [f"ob{oi}"])
                dst = (KT if isk else QT)[s, h, :, t0:t0 + 512]
                S.st(dst, ob[oi][:], r=[f"ob{oi}"])
            for (col0, DST) in [(1024, VV), (3072, RI)]:
                for b in range(4):
                    pi = gi % 4
                    gi += 1
                    proj_tm(col0, b, pi)
                    oi = obi % 4
                    obi += 1
                    S.dve(lambda e: e.tensor_copy(out=ob[oi][:], in_=pq[pi][:]), [f"pq{pi}"], [f"ob{oi}"])
                    S.st(DST[s, :, :, tt * 4 + b, :].rearrange("h p v -> p h v"),
                         ob[oi][:].rearrange("p (h v) -> p h v", h=4), r=[f"ob{oi}"])
            for (col0, DST) in [(1536, RQ), (3584, RG)]:
                for h in range(4):
                    pi = gi % 4
                    gi += 1
                    proj_fm(col0 + h * 128, pi)
                    oi = obi % 4
                    obi += 1
                    S.act(lambda e: e.activation(out=ob[oi][:], in_=pq[pi][:], func=AF.Silu), [f"pq{pi}"], [f"ob{oi}"])
                    S.st(DST[s, h, :, t0:t0 + 512], ob[oi][:], r=[f"ob{oi}"])
            for d in range(2):
                for h in range(4):
                    pi = gi % 4
                    fi = gi % 2
                    gi += 1
                    proj_fm(2048 + d * 512 + h * 128, pi)
                    S.act(lambda e: e.activation(out=of32[fi][:], in_=pq[pi][:], func=AF.Sigmoid), [f"pq{pi}"], [f"of{fi}"])
                    S.dve(lambda e: e.tensor_scalar(out=of32[fi][:], in0=of32[fi][:], scalar1=olb[:, d, h:h + 1],
                                                    scalar2=lbv[:, d, h:h + 1], op0=ALU.mult, op1=ALU.add),
                          [f"of{fi}", "olb", "lbv"], [f"of{fi}"])
                    S.act(lambda e: e.activation(out=lf[fi][:], in_=of32[fi][:], func=AF.Ln), [f"of{fi}"], [f"lf{fi}"])
                    S.st(LOGF[s, d, h, :, t0:t0 + 512], lf[fi][:], r=[f"lf{fi}"])
                    oi = obi % 4
                    obi += 1
                    S.pool(lambda e: e.tensor_scalar(out=ob[oi][:], in0=of32[fi][:], scalar1=-1.0, scalar2=1.0,
                                                     op0=ALU.mult, op1=ALU.add), [f"of{fi}"], [f"ob{oi}"])
                    S.st(KK[s, d, h, :, t0:t0 + 512], ob[oi][:], r=[f"ob{oi}"])
    sc.close()

    if upto == 1:
        return nc
    sc = Scope()
    Kt = [[sc.sb(f"K{p}{j}", [128, L], BF16) for j in range(2)] for p in range(2)]
    Vt = [sc.sb(f"V{p}", [128, NB, 128], BF16) for p in range(2)]
    QS = [[[sc.sb(f"QS{p}{v}{j}", [128, 512], BF16) for j in range(2)] for v in range(3)] for p in range(2)]
    bbase = sc.sb("bbase", [128, 4, 512], F32)
    sd = sc.sb("sd", [128, 1024], F32)
    pt = [sc.sb(f"pt{i}", [128, 1024], BF16) for i in range(4)]
    zacc = [[sc.sb(f"zacc{p}{j}", [128, 512], F32) for j in range(2)] for p in range(2)]
    zh = [sc.sb(f"zh{j}", [128, 512], BF16) for j in range(2)]
    zl = [sc.sb(f"zl{j}", [128, 512], BF16) for j in range(2)]
    ev = [[sc.sb(f"ev{p}{i}", [128, 512], F32) for i in range(4)] for p in range(2)]
    ao = sc.sb("ao", [128, 512], F32)
    asq = sc.sb("asq", [128, 512], BF16)
    at1 = sc.sb("at1", [128, 512], F32)
    ars = sc.sb("ars", [128, 512], F32)
    mo = [sc.sb(f"mo{i}", [128, 512], BF16) for i in range(2)]
    scp = [sc.ps(f"scp{i}", [128, 1024]) for i in range(3)]
    acc = [sc.ps(f"acc{i}", [128, 512]) for i in range(2)]
    for i in range(4):
        S.ld(bbase[:, i, :], c_bbase[i, :, :], w=["bbase"])
    for p in range(2):
        for j in range(2):
            S.pool(lambda e: e.memset(Kt[p][j][:], 0.0), [], [f"K{p}"])
            for v in range(3):
                S.pool(lambda e: e.memset(QS[p][v][j][:], 0.0), [], [f"QS{p}"])
    pair = 0
    qkc = 0
    hcount = 0
    qcount = 0
    evc = 0
    deferred = []

    def epilogue_tail(s, h, q0, E4):
        nonlocal pair
        ek = [f"ev{E4}{i}" for i in range(4)]
        EV = ev[E4]
        pz = pair % 3
        pair += 1
        for j in range(2):
            eng = S.dve if j == 0 else S.pool
            eng(lambda e: e.tensor_copy(out=zh[j][:], in_=zacc[E4][j][:]), [f"zacc{E4}{j}"], [f"zh{j}"])
            eng(lambda e: e.tensor_tensor(out=zl[j][:], in0=zacc[E4][j][:], in1=zh[j][:], op=ALU.subtract),
                [f"zacc{E4}{j}", f"zh{j}"], [f"zl{j}"])
            S.pe(lambda e: e.matmul(scp[pz][:, j * 512:(j + 1) * 512], lhsT=onesb[:], rhs=zh[j][:], start=True, stop=False),
                 ["onesb", f"zh{j}"], [f"scp{pz}"], inc=False)
            S.pe(lambda e: e.matmul(scp[pz][:, j * 512:(j + 1) * 512], lhsT=onesb[:], rhs=zl[j][:], start=False, stop=True),
                 ["onesb", f"zl{j}"], [f"scp{pz}"])
        for j in range(2):
            S.dve(lambda e: e.reciprocal(out=EV[2 + j][:], in_=scp[pz][:, j * 512:(j + 1) * 512]), [f"scp{pz}"], [ek[2 + j]])
            S.dve(lambda e: e.tensor_tensor(out=EV[j][:], in0=EV[j][:], in1=EV[2 + j][:], op=ALU.mult),
                  [ek[j], ek[2 + j]], [ek[j]])
        S.dve(lambda e: e.scalar_tensor_tensor(out=ao[:], in0=EV[1][:], scalar=nlam[:, 0:1], in1=EV[0][:],
                                               op0=ALU.mult, op1=ALU.add), [ek[0], ek[1], "nlam"], ["ao"])
        S.act(lambda e: e.activation(out=asq[:], in_=ao[:], func=AF.Square), ["ao"], ["asq"])
        pi = pair % 3
        pair += 1
        S.pe(lambda e: e.matmul(scp[pi][:, 0:512], lhsT=onesb[:], rhs=asq[:], start=True, stop=True),
             ["asq", "onesb"], [f"scp{pi}"])
        rstd_from(sc, scp[pi][:, 0:512], 128, ars[:], at1[:], [f"scp{pi}"], ["at1", "ars"])
        mi = (q0 // 512) % 2
        S.dve(lambda e: e.scalar_tensor_tensor(out=mo[mi][:], in0=ao[:], scalar=aonw[:, 0:1], in1=ars[:],
                                               op0=ALU.mult, op1=ALU.mult), ["ao", "ars", "aonw"], [f"mo{mi}"])
        S.st(MIXT[s, h, :, q0:q0 + 512], mo[mi][:], r=[f"mo{mi}"])

    for s in range(0 if SKIP123 else NSEQ):
        for h in range(NH):
            hp = hcount % 2
            hcount += 1
            sl = SLOPES[h]
            for j in range(2):
                S.ld(Kt[hp][j][0:64, :], KT[s, h, j * 64:(j + 1) * 64, :], w=[f"K{hp}"])
                S.ld(Kt[hp][j][64:69, :], c_kaug[:, :], w=[f"K{hp}"])
            S.ld(Vt[hp][:], VV[s, h, :, :, :], w=[f"V{hp}"])
            for qt in range(NT):
                q0 = qt * 512
                qp = qcount % 2
                qcount += 1
                for v in range(3):
                    for j in range(2):
                        S.ld(QS[qp][v][j][0:64, :], QT[s, h, j * 64:(j + 1) * 64, q0:q0 + 512], w=[f"QS{qp}"])
                        if v < 2:
                            S.ld(QS[qp][v][j][64:69, :], c_qaug[h, v, :, q0:q0 + 512], w=[f"QS{qp}"])
                kbs = []
                for kb in range(NB):
                    k0 = kb * 128
                    if k0 + 127 < q0:
                        dmin = q0 - (k0 + 127)
                        typ = 0
                    elif k0 >= q0 + 512:
                        dmin = k0 - (q0 + 511)
                        typ = 1
                    else:
                        dmin = 0
                        typ = 2
                    if dmin * sl > SKIP_T:
                        continue
                    kbs.append((kb, typ))
                nk = len(kbs)
                slots = []

                def emit_qk(idx):
                    nonlocal pair
                    kb, typ = kbs[idx]
                    nonlocal qkc
                    pi = pair % 3
                    pair += 1
                    pti = qkc % 4
                    qkc += 1
                    slots.append(pti)
                    k0 = kb * 128
                    for j in range(2):
                        S.pe(lambda e: e.matmul(scp[pi][:, j * 512:(j + 1) * 512], lhsT=Kt[hp][j][:, k0:k0 + 128],
                                                rhs=QS[qp][typ][j][:], start=True, stop=True),
                             [f"K{hp}", f"QS{qp}"], [f"scp{pi}"], inc=(j == 1))
                    if typ == 2:
                        i = kb - 4 * qt
                        for j in range(2):
                            S.dve(lambda e: e.scalar_tensor_tensor(out=sd[:, j * 512:(j + 1) * 512], in0=bbase[:, i, :], scalar=sl,
                                                                   in1=scp[pi][:, j * 512:(j + 1) * 512], op0=ALU.mult, op1=ALU.add),
                                  ["bbase", f"scp{pi}"], ["sd"])
                        S.act(lambda e: e.activation(out=pt[pti][:], in_=sd[:], func=AF.Exp), ["sd"], [f"pt{pti}"])
                    else:
                        S.act(lambda e: e.activation(out=pt[pti][:], in_=scp[pi][:], func=AF.Exp), [f"scp{pi}"], [f"pt{pti}"])

                def emit_pv(idx):
                    kb, typ = kbs[idx]
                    pti = slots[idx]
                    first = idx == 0
                    last = idx == nk - 1
                    for j in range(2):
                        S.pe(lambda e: e.matmul(acc[j][:], lhsT=Vt[hp][:, kb, :], rhs=pt[pti][:, j * 512:(j + 1) * 512],
                                                start=first, stop=last), [f"V{hp}", f"pt{pti}"], [f"acc{j}"], inc=(j == 1))
                    for j in range(2):
                        eng = S.dve if j == 0 else S.pool
                        if first:
                            eng(lambda e: e.tensor_copy(out=zacc[E4][j][:], in_=pt[pti][:, j * 512:(j + 1) * 512]),
                                [f"pt{pti}"], [f"zacc{E4}{j}"])
                        else:
                            eng(lambda e: e.tensor_tensor(out=zacc[E4][j][:], in0=zacc[E4][j][:], in1=pt[pti][:, j * 512:(j + 1) * 512],
                                                          op=ALU.add), [f"pt{pti}", f"zacc{E4}{j}"], [f"zacc{E4}{j}"])

                E4 = evc % 2
                evc += 1
                for idx in range(nk):
                    emit_qk(idx)
                    if idx >= 2:
                        emit_pv(idx - 2)
                    if idx == 4 and deferred:
                        epilogue_tail(*deferred.pop(0))
                for idx in range(max(nk - 2, 0), nk):
                    emit_pv(idx)
                while deferred:
                    epilogue_tail(*deferred.pop(0))
                for i in range(2):
                    S.dve(lambda e: e.tensor_copy(out=ev[E4][i][:], in_=acc[i][:]), [f"acc{i}"], [f"ev{E4}{i}"])
                deferred.append((s, h, q0, E4))
    while deferred:
        epilogue_tail(*deferred.pop(0))
    sc.close()

    if upto == 2:
        return nc
    sc = Scope()
    rq = sc.sb("rq", [128, L], BF16)
    rg = sc.sb("rg", [128, L], BF16)
    ri = sc.sb("ri", [128, NB, 128], BF16)
    lgf = sc.sb("lgf", [128, L], F32)
    kkt = sc.sb("kkt", [128, L], BF16)
    ofw = sc.sb("ofw", [128, L], F32)
    Pt = [sc.sb(f"Pt{i}", [128, 128], F32) for i in range(2)]
    Gt = sc.sb("Gt", [128, 128], F32)
    rsm = [sc.sb(f"rsm{i}", [128, 4], F32) for i in range(2)]
    E = [[sc.sb(f"E{i}{k}", [128, 128], F32) for k in range(3)] for i in range(2)]
    qg = [sc.sb(f"qg{i}", [128, 128], BF16) for i in range(2)]
    kg = [sc.sb(f"kg{i}", [128, 128], BF16) for i in range(2)]
    qs = [sc.sb(f"qs{i}", [128, 128], BF16) for i in range(2)]
    ATs = [sc.sb(f"AT{i}", [128, 128], BF16) for i in range(2)]
    Acl = [sc.sb(f"Acl{i}", [128, 128], F32) for i in range(2)]
    kgt = [sc.sb(f"kgt{i}", [128, 128], BF16) for i in range(2)]
    Sf = sc.sb("Sf", [128, 128], F32)
    Sb = [sc.sb(f"Sb{i}", [128, 128], BF16) for i in range(2)]
    wtmp = sc.sb("wtmp", [128, 128], F32)
    osum = sc.sb("osum", [128, 512], F32)
    osq = sc.sb("osq", [128, 512], BF16)
    ot1 = sc.sb("ot1", [128, 512], F32)
    ors = sc.sb("ors", [128, 512], F32)
    omo = [sc.sb(f"omo{i}", [128, 512], BF16) for i in range(2)]
    pA = [sc.ps(f"pA{i}", [128, 512])[:, 0:128] for i in range(2)]
    pT = [sc.ps(f"pT{i}", [128, 1024], BF16)[:, 0:128] for i in range(2)]
    pO = [sc.ps(f"pO{i}", [128, 512])[:, 0:128] for i in range(2)]
    pW = sc.ps("pW", [128, 512])[:, 0:128]
    pN = sc.ps("pN", [128, 512])
    step = 0
    ocount = 0
    PIECE = 2048 if L >= 2048 else L
    for s in range(0 if SKIP123 else NSEQ):
        for h in range(NH):
            for c0 in range(0, L, PIECE):
                S.ld(rq[:, c0:c0 + PIECE], RQ[s, h, :, c0:c0 + PIECE], w=["rq"])
                S.ld(rg[:, c0:c0 + PIECE], RG[s, h, :, c0:c0 + PIECE], w=["rg"])
            S.ld(ri[:], RI[s, h, :, :, :], w=["ri"])
            for d in range(2):
                for c0 in range(0, L, PIECE):
                    S.ld(lgf[:, c0:c0 + PIECE], LOGF[s, d, h, :, c0:c0 + PIECE], w=["lgf"])
                    S.ld(kkt[:, c0:c0 + PIECE], KK[s, d, h, :, c0:c0 + PIECE], w=["kkt"])
                mask = maskf if d == 0 else maskb
                order = list(range(NB)) if d == 0 else list(range(NB - 1, -1, -1))
                par = {}

                def prep(ci):
                    nonlocal step
                    n = order[ci]
                    c0 = n * 128
                    i2 = step % 2
                    step += 1
                    par[ci] = i2
                    P, PK = Pt[i2], f"Pt{i2}"
                    sl_ = slice(c0, c0 + 128)
                    if d == 0:
                        S.dve(lambda e: e.tensor_tensor_scan(out=P[:], data0=onesf[:], data1=lgf[:, sl_], initial=0.0,
                                                             op0=ALU.mult, op1=ALU.add), ["lgf", "onesf"], [PK])
                        lastc = 127
                    else:
                        S.dve(lambda e: e.tensor_tensor_scan(out=Gt[:], data0=onesf[:], data1=lgf[:, sl_], initial=0.0,
                                                             op0=ALU.mult, op1=ALU.add), ["lgf", "onesf"], ["Gt"])
                        S.pool(lambda e: e.tensor_tensor(out=P[:], in0=lgf[:, sl_], in1=Gt[:], op=ALU.subtract), ["lgf", "Gt"], [PK])
                        S.dve(lambda e: e.tensor_scalar(out=P[:], in0=P[:], scalar1=Gt[:, 127:128], scalar2=None, op0=ALU.add),
                              [PK, "Gt"], [PK])
                        lastc = 0
                    R, RK = rsm[i2], f"rsm{i2}"
                    S.dve(lambda e: e.tensor_scalar(out=R[:, 0:1], in0=P[:, 63:64], scalar1=-1.0, scalar2=None, op0=ALU.mult), [PK], [RK])
                    S.act(lambda e: e.activation(out=E[i2][0][:], in_=P[:], func=AF.Exp, bias=R[:, 0:1]), [PK, RK], [f"E{i2}0"])
                    S.act(lambda e: e.activation(out=E[i2][1][:], in_=P[:], func=AF.Exp, scale=-1.0, bias=P[:, 63:64]), [PK], [f"E{i2}1"])
                    S.act(lambda e: e.activation(out=E[i2][2][:], in_=P[:], func=AF.Exp), [PK], [f"E{i2}2"])
                    S.act(lambda e: e.activation(out=R[:, 1:2], in_=P[:, lastc:lastc + 1], func=AF.Exp), [PK, RK], [RK])
                    S.act(lambda e: e.activation(out=R[:, 2:3], in_=P[:, lastc:lastc + 1], func=AF.Exp, bias=R[:, 0:1]), [PK, RK], [RK])
                    S.pool(lambda e: e.tensor_tensor(out=qg[i2][:], in0=rq[:, sl_], in1=E[i2][0][:], op=ALU.mult), ["rq", f"E{i2}0"], [f"qg{i2}"])
                    S.dve(lambda e: e.tensor_tensor(out=kg[i2][:], in0=kkt[:, sl_], in1=E[i2][1][:], op=ALU.mult), ["kkt", f"E{i2}1"], [f"kg{i2}"])
                    S.pool(lambda e: e.tensor_tensor(out=qs[i2][:], in0=rq[:, sl_], in1=E[i2][2][:], op=ALU.mult), ["rq", f"E{i2}2"], [f"qs{i2}"])
                    S.pe(lambda e: e.matmul(pA[i2][:], lhsT=kg[i2][:], rhs=qg[i2][:], start=True, stop=True),
                         [f"kg{i2}", f"qg{i2}"], [f"pA{i2}"])
                    S.dve(lambda e: e.tensor_scalar(out=Acl[i2][:], in0=pA[i2][:], scalar1=1e30, scalar2=-1e30, op0=ALU.min, op1=ALU.max),
                          [f"pA{i2}"], [f"Acl{i2}"])
                    S.pool(lambda e: e.tensor_tensor(out=ATs[i2][:], in0=Acl[i2][:], in1=mask[:], op=ALU.mult),
                           [f"Acl{i2}", "maskf", "maskb"], [f"AT{i2}"])
                    S.pe(lambda e: e.transpose(out=pT[i2][:], in_=kg[i2][:], identity=identb[:]), [f"kg{i2}"], [f"pT{i2}"])
                    S.act(lambda e: e.copy(out=kgt[i2][:], in_=pT[i2][:]), [f"pT{i2}"], [f"kgt{i2}"])

                def chain(ci):
                    nonlocal ocount
                    n = order[ci]
                    c0 = n * 128
                    i2 = par[ci]
                    sl_ = slice(c0, c0 + 128)
                    R, RK = rsm[i2], f"rsm{i2}"
                    sbi = (ci + 1) % 2
                    S.pe(lambda e: e.matmul(pO[i2][:], lhsT=ri[:, n, :], rhs=ATs[i2][:], start=True, stop=(ci == 0)),
                         ["ri", f"AT{i2}"], [f"pO{i2}"], inc=(ci == 0))
                    if ci > 0:
                        S.pe(lambda e: e.matmul(pO[i2][:], lhsT=Sb[ci % 2][:], rhs=qs[i2][:], start=False, stop=True),
                             [f"Sb{ci % 2}", f"qs{i2}"], [f"pO{i2}"])
                    S.pe(lambda e: e.matmul(pW[:], lhsT=kgt[i2][:], rhs=ri[:, n, :], start=True, stop=True),
                         [f"kgt{i2}", "ri"], ["pW"])
                    if ci == 0:
                        S.dve(lambda e: e.tensor_scalar(out=Sf[:], in0=pW[:], scalar1=R[:, 2:3], scalar2=None, op0=ALU.mult),
                              ["pW", RK], ["Sf"])
                    else:
                        S.dve(lambda e: e.tensor_scalar(out=wtmp[:], in0=pW[:], scalar1=R[:, 2:3], scalar2=None, op0=ALU.mult),
                              ["pW", RK], ["wtmp"])
                        S.dve(lambda e: e.scalar_tensor_tensor(out=Sf[:], in0=Sf[:], scalar=R[:, 1:2], in1=wtmp[:],
                                                               op0=ALU.mult, op1=ALU.add), ["Sf", "wtmp", RK], ["Sf"])
                    S.pool(lambda e: e.tensor_copy(out=Sb[sbi][:], in_=Sf[:]), ["Sf"], [f"Sb{sbi}"])
                    if d == 0:
                        S.act(lambda e: e.copy(out=ofw[:, sl_], in_=pO[i2][:]), [f"pO{i2}"], ["ofw"])
                    else:
                        g4 = n % 4
                        S.dve(lambda e: e.tensor_tensor(out=osum[:, g4 * 128:(g4 + 1) * 128], in0=pO[i2][:], in1=ofw[:, sl_], op=ALU.add),
                              [f"pO{i2}", "ofw"], ["osum"])
                        if g4 == 0:
                            b0 = n * 128
                            S.act(lambda e: e.activation(out=osq[:], in_=osum[:], func=AF.Square), ["osum"], ["osq"])
                            S.pe(lambda e: e.matmul(pN[:], lhsT=onesb[:], rhs=osq[:], start=True, stop=True), ["osq", "onesb"], ["pN"])
                            rstd_from(sc, pN[:], 128, ors[:], ot1[:], ["pN"], ["ot1", "ors"])
                            S.dve(lambda e: e.scalar_tensor_tensor(out=ot1[:], in0=osum[:], scalar=ronw[:, 0:1], in1=ors[:],
                                                                   op0=ALU.mult, op1=ALU.mult), ["osum", "ors", "ronw", "ot1"], ["ot1"])
                            oi = ocount % 2
                            ocount += 1
                            S.pool(lambda e: e.tensor_tensor(out=omo[oi][:], in0=ot1[:], in1=rg[:, b0:b0 + 512], op=ALU.mult),
                                   ["ot1", "rg"], [f"omo{oi}"])
                            S.st(MIXT[s, 4 + h, :, b0:b0 + 512], omo[oi][:], r=[f"omo{oi}"])

                prep(0)
                for ci in range(NB):
                    if ci + 1 < NB:
                        prep(ci + 1)
                    chain(ci)
    sc.close()

    if upto == 3:
        return nc
    W = FW
    NBW = W // 128
    sc = Scope()
    wo = sc.sb("wo", [128, 8, D], BF16)
    wu = sc.sb("wu", [128, 8, 2 * DFF], BF16)
    for c in range(8):
        S.ld(wo[:, c, :], WOUT[:, c, :], w=["wo"])
        S.ld(wu[:, c, :], WUP[:, c, :], w=["wu"])
    wd = [sc.sb(f"wd{i}", [128, 22, 128], BF16) for i in range(2)]
    xq = [sc.sb(f"xq{i}", [128, NBW, 128], F32) for i in range(4)]
    mx = sc.sb("mx", [128, 8, W], BF16)
    x1T = sc.sb("x1T", [128, 8, W], F32)
    fsq = [sc.sb(f"fsq{i}", [128, W], BF16) for i in range(2)]
    ft1 = sc.sb("ft1", [128, W], F32)
    frs = sc.sb("frs", [128, W], F32)
    h2T = sc.sb("h2T", [128, 8, W], BF16)
    big = sc.sb("big", [128, 8, W], F32)
    sg = [sc.sb(f"sg{i}", [128, W], F32) for i in range(2)]
    gT = sc.sb("gT", [128, 22, W], BF16)
    yo = sc.sb("yo", [128, D], F32)
    Bk = [sc.ps(f"B{i}", [128, 512]) for i in range(6)]
    pY = sc.ps("pY", [128, 2, 512])
    bank = [(Bk[i][:, 0:W], f"B{i}") for i in range(6)] + [(pY[:, 0, 0:W], "B6"), (pY[:, 1, 0:W], "B7")]
    p1 = [bank[0][0], bank[1][0]]
    pS = bank[2][0]
    pui = 0
    for i in range(4):
        S.pool(lambda e: e.memset(xq[i][:], 0.0), [], [f"xq{i}"])
    S.pool(lambda e: e.memset(mx[:], 0.0), [], ["mx"])
    FTt = W - 2
    starts = list(range(0, L - FTt, FTt)) + [L - FTt]
    ui = 0
    wdi = 0
    xqi = 0
    for s in range(NSEQ):
        for t0 in starts:
            w0 = t0 - 1
            lo, hi = max(w0, 0), min(w0 + W, L)
            if P4SUB < 2:
                continue
            S.ld(mx[:, :, lo - w0:hi - w0], MIXT[s, :, :, lo:hi].rearrange("c p t -> p c t"), w=["mx"])
            for c in range(8):
                pi = c % 2
                xi = xqi % 4
                xqi += 1
                for b in range(NBW):
                    r0, r1 = max(w0 + b * 128, lo), min(w0 + (b + 1) * 128, hi)
                    S.ld(xq[xi][r0 - (w0 + b * 128):r1 - (w0 + b * 128), b, :], x[s, r0:r1, c * 128:(c + 1) * 128], w=[f"xq{xi}"])
                if P4SUB < 3:
                    continue
                for k in range(8):
                    S.pe(lambda e: e.matmul(p1[pi][:], lhsT=wo[:, k, c * 128:(c + 1) * 128], rhs=mx[:, k, :],
                                            start=(k == 0), stop=False), ["wo", "mx"], [f"B{pi}"], inc=False)
                for b in range(NBW):
                    S.pe(lambda e: e.matmul(p1[pi][:, b * 128:(b + 1) * 128], lhsT=xq[xi][:, b, :], rhs=identf[:],
                                            start=False, stop=(b == NBW - 1)), [f"xq{xi}", "identf"], [f"B{pi}"], inc=(b == NBW - 1))
                if P4SUB < 4:
                    continue
                S.dve(lambda e: e.tensor_copy(out=x1T[:, c, :], in_=p1[pi][:]), [f"B{pi}"], ["x1T"])
                if P4SUB == 4 and os.environ.get("P4V") == "copyonly":
                    continue
                S.act(lambda e: e.activation(out=fsq[pi][:], in_=x1T[:, c, :], func=AF.Square), ["x1T"], [f"fsq{pi}"])
                if P4SUB == 4 and os.environ.get("P4V") == "nosum":
                    continue
                S.pe(lambda e: e.matmul(pS[:], lhsT=onesb[:], rhs=fsq[pi][:], start=(c == 0), stop=(c == 7)),
                     [f"fsq{pi}", "onesb"], ["B2"], inc=(c == 7))
            if P4SUB < 5:
                continue
            rstd_from(sc, pS[:], D, frs[:], ft1[:], ["B2"], ["ft1", "frs"])
            for c in range(8):
                eng = S.dve if c % 2 == 0 else S.pool
                eng(lambda e: e.tensor_tensor(out=h2T[:, c, :], in0=x1T[:, c, :], in1=frs[:], op=ALU.mult), ["x1T", "frs"], ["h2T"])
            if w0 < 0:
                S.dve(lambda e: e.memset(h2T[:, :, 0:1], 0.0), ["h2T"], ["h2T"])
            if w0 + W > L:
                S.dve(lambda e: e.memset(h2T[:, :, W - 1:W], 0.0), ["h2T"], ["h2T"])
            if P4STAGE < 1:
                continue
            a0, a1 = 1, W - 1
            for j in range(22):
                u2 = ui % 2
                ui += 1
                pis = []
                for k2, uc in enumerate([j, 22 + j]):
                    PU, PK = bank[pui % 8]
                    pui += 1
                    pis.append((PU, PK))
                    for k in range(8):
                        S.pe(lambda e: e.matmul(PU, lhsT=wu[:, k, uc * 128:(uc + 1) * 128], rhs=h2T[:, k, :],
                                                start=(k == 0), stop=(k == 7)), ["wu", "h2T"], [PK], inc=(k == 7))
                for k2, uc in enumerate([j, 22 + j]):
                    PU, PK = pis[k2]
                    cs = 4 + u2 * 2 + k2
                    S.act(lambda e: e.activation(out=big[:, cs, a0:a1], in_=PU[:, a0 - 1:a1 - 1], func=AF.Identity,
                                                 scale=cw[:, uc, 0:1], bias=cb[:, uc:uc + 1]), [PK, "cw", "cb"], [f"big{cs}"])
                for tap in (1, 2):
                    for k2, uc in enumerate([j, 22 + j]):
                        PU, PK = pis[k2]
                        cs = 4 + u2 * 2 + k2
                        S.dve(lambda e: e.scalar_tensor_tensor(out=big[:, cs, a0:a1], in0=PU[:, a0 - 1 + tap:a1 - 1 + tap],
                                                               scalar=cw[:, uc, tap:tap + 1], in1=big[:, cs, a0:a1],
                                                               op0=ALU.mult, op1=ALU.add), [PK, f"big{cs}", "cw"], [f"big{cs}"])
                cg, cu = 4 + u2 * 2, 4 + u2 * 2 + 1
                S.act(lambda e: e.activation(out=sg[u2][:, 1:W - 1], in_=big[:, cg, 1:W - 1], func=AF.Silu), [f"big{cg}"], [f"sg{u2}"])
                S.dve(lambda e: e.tensor_tensor(out=gT[:, j, 0:FTt], in0=sg[u2][:, 1:W - 1], in1=big[:, cu, 1:W - 1], op=ALU.mult),
                      [f"sg{u2}", f"big{cu}"], ["gT"])
            if P4STAGE < 2:
                continue
            for c in range(8):
                wi = wdi % 2
                wdi += 1
                S.ld(wd[wi][:], WDN[c, :, :, :], w=[f"wd{wi}"])
                PU, PK = bank[pui % 6]
                pui += 1
                for j in range(22):
                    S.pe(lambda e: e.matmul(PU[:, 0:FTt], lhsT=wd[wi][:, j, :], rhs=gT[:, j, 0:FTt], start=(j == 0), stop=(j == 21)),
                         [f"wd{wi}", "gT"], [PK], inc=(j == 21))
                S.dve(lambda e: e.tensor_tensor(out=big[:, c, 0:FTt], in0=PU[:, 0:FTt], in1=x1T[:, c, 1:W - 1], op=ALU.add),
                      [PK, "x1T"], [f"big{c}"])
            if P4STAGE < 3:
                continue
            for b in range(NBW):
                nb = 128 if b < NBW - 1 else FTt - 128 * (NBW - 1)
                for c in range(8):
                    S.pe(lambda e: e.transpose(out=pY[0:nb, c // 4, (c % 4) * 128:(c % 4 + 1) * 128], in_=big[:, c, b * 128:b * 128 + nb],
                                               identity=identf[:]), [f"big{c}", "identf"], [f"B{6 + c // 4}"], inc=(c % 4 == 3))
                S.act(lambda e: e.copy(out=yo[0:nb, 0:512], in_=pY[0:nb, 0, :]), ["B6"], ["yo"])
                S.dve(lambda e: e.tensor_copy(out=yo[0:nb, 512:1024], in_=pY[0:nb, 1, :]), ["B7"], ["yo"])
                S.st(y[s, t0 + b * 128:t0 + b * 128 + nb, :], yo[0:nb, :], r=["yo"])
    sc.close()
    return nc


def host_params(inp):
    f = np.float32
    g = lambda k: np.asarray(inp[k], dtype=f)
    P = {}
    P["w_in"] = np.ascontiguousarray(g("w_in")[0])
    P["w_out"] = np.ascontiguousarray(g("w_out")[0])
    P["w_up"] = np.ascontiguousarray(g("w_up")[0])
    P["w_down"] = np.ascontiguousarray(g("w_down")[0])
    P["p_nmw"] = np.ascontiguousarray(g("norm_mix_w")[0].reshape(8, 128).T)
    P["p_nfw"] = np.ascontiguousarray(g("norm_ffn_w")[0].reshape(8, 128).T)
    P["p_qnw"] = np.ascontiguousarray(np.tile(g("q_norm_w")[0], 2).reshape(128, 1))
    P["p_knw"] = np.ascontiguousarray(np.tile(g("k_norm_w")[0], 2).reshape(128, 1))
    lam = np.stack([g("lambda_q1")[0], g("lambda_k1")[0], g("lambda_q2")[0], g("lambda_k2")[0]])
    P["p_lam"] = np.ascontiguousarray(np.broadcast_to(lam[None], (128, 4, 64)))
    P["p_aonw"] = np.ascontiguousarray(g("attn_out_norm_w")[0].reshape(128, 1))
    P["p_ronw"] = np.ascontiguousarray(g("rec_out_norm_w")[0].reshape(128, 1))
    lb = np.stack([g("lb_fwd"), g("lb_bwd")])
    lb = lb.reshape(2, 2, NH, 128)
    P["p_lb"] = np.ascontiguousarray(lb.transpose(3, 0, 2, 1))
    P["p_cw"] = np.ascontiguousarray(g("conv_w")[0].reshape(3, 44, 128).transpose(2, 1, 0))
    P["p_cb"] = np.ascontiguousarray(g("conv_b")[0].reshape(44, 128).T)
    return P


_CACHE = {}


def kernel(**inputs):
    xp = np.asarray(inputs["x_prompt"], dtype=np.float32)
    xs = np.asarray(inputs["x_sample"], dtype=np.float32)
    L = xs.shape[1]
    ncores = 8
    NSEQ = 2
    P = host_params(inputs)
    P.update(host_consts(L))
    in_maps = []
    for c in range(ncores):
        m = dict(P)
        m["x"] = np.ascontiguousarray(np.stack([xs[c], xp[c % 2]]))
        in_maps.append(m)
    key = (L, NSEQ)
    if key not in _CACHE:
        _CACHE[key] = build(L, NSEQ)
    nc = _CACHE[key]
    res = run_bass_kernel_spmd(nc, in_maps, core_ids=list(range(ncores)))
    y_sample = np.stack([np.asarray(res.results[c]["y"][0], dtype=np.float32) for c in range(ncores)])
    y_prompt = np.stack([np.asarray(res.results[c]["y"][1], dtype=np.float32) for c in range(2)])
    return (y_prompt, y_sample)
```

```python
import os
import numpy as np
import ml_dtypes
import concourse.bass as bass
import concourse.mybir as mybir
from concourse.bass_utils import run_bass_kernel_spmd

F32 = mybir.dt.float32
BF16 = mybir.dt.bfloat16
AF = mybir.ActivationFunctionType
ALU = mybir.AluOpType

D = 1024
DFF = 2816
NH = 4
EPS = 1e-6
SLOPES = [2.0 ** (-2 * (i + 1)) for i in range(NH)]
LAM_INIT = 0.8 - 0.6 * 1.0
SKIP_T = 50.0
P4STAGE = int(os.environ.get('P4STAGE', '9'))
P4SUB = int(os.environ.get('P4SUB', '9'))
SKIP123 = int(os.environ.get('SKIP123', '0'))
FW = 256


class Sched:
    EPOCH = int(os.environ.get("EPOCH", "12000"))

    def __init__(self, nc, ndma=10):
        self.nc = nc
        self.eng = {"pe": nc.tensor, "act": nc.scalar, "dve": nc.vector, "pool": nc.gpsimd, "sp": nc.sync}
        self.sem = {}
        self.cnt = {}
        self.cur = {}
        self.nsem = 0
        self.waited = {k: {} for k in self.eng}
        for k in ["pe", "act", "dve", "pool"]:
            self._new_epoch(k)
        self.dring = {}
        for q in ["sp", "pool"]:
            self.dring[q] = [[self._new_dsem(q) for _ in range(ndma)], 0]
        self.lastw = {}
        self.lastr = {}
        self.pending = {k: [] for k in self.eng}

    def _alloc(self, name):
        cm = self.nc.semaphore(name)
        s = cm.__enter__()
        self.nsem += 1
        return s

    def _new_epoch(self, e):
        key = f"{e}#{self.nsem}"
        self.sem[key] = self._alloc("s_" + key.replace("#", "_"))
        self.cur[e] = key
        self.cnt[key] = 0

    def _new_dsem(self, q):
        key = f"d{q}#{self.nsem}"
        self.sem[key] = self._alloc("s_" + key.replace("#", "_"))
        return [key, 0]

    def _wait(self, e, semkey, val):
        if val <= 0:
            return
        w = self.waited[e]
        if w.get(semkey, 0) >= val:
            return
        self.eng[e].wait_ge(self.sem[semkey], val)
        w[semkey] = val

    def _deps(self, e, reads, writes):
        skip = self.cur[e].split("#")[0] if e == "pe" else None
        def need(tok):
            if skip is not None and tok[0].split("#")[0] == skip:
                return
            self._wait(e, tok[0], tok[1])
        for b in reads:
            if b in self.lastw:
                need(self.lastw[b])
        for b in writes:
            if b in self.lastw:
                need(self.lastw[b])
            for r in self.lastr.get(b, ()):
                need(r)

    def _commit(self, tok, reads, writes):
        for b in writes:
            self.lastw[b] = tok
            self.lastr[b] = []
        for b in reads:
            lst = self.lastr.setdefault(b, [])
            lst[:] = [t for t in lst if t[0] != tok[0]]
            lst.append(tok)

    def op(self, e, fn, reads=(), writes=(), inc=True):
        self._deps(e, reads, writes)
        ins = fn(self.eng[e])
        key = self.cur[e]
        if inc:
            self.cnt[key] += 1
            ins.then_inc(self.sem[key], 1)
            tok = (key, self.cnt[key])
            self._commit(tok, reads, writes)
            if self.cnt[key] >= self.EPOCH:
                self._new_epoch(e)
        else:
            tok = (key, self.cnt[key] + 1)
            self._commit(tok, reads, writes)
        return ins

    def pe(self, fn, r=(), w=(), inc=True):
        return self.op("pe", fn, r, w, inc)

    def act(self, fn, r=(), w=()):
        return self.op("act", fn, r, w)

    def dve(self, fn, r=(), w=()):
        return self.op("dve", fn, r, w)

    def pool(self, fn, r=(), w=()):
        return self.op("pool", fn, r, w)

    def dma(self, q, out, in_, reads=(), writes=()):
        ring, pos = self.dring[q]
        slot = ring[pos]
        if slot[1] >= int(os.environ.get("DROLL", "900")):
            self._wait(q, slot[0], 16 * slot[1])
            ring[pos] = slot = self._new_dsem(q)
        self.dring[q][1] = (pos + 1) % len(ring)
        key, n = slot
        self._wait(q, key, 16 * n)
        self._deps(q, reads, writes)
        self.eng[q].dma_start(out=out, in_=in_).then_inc(self.sem[key], 16)
        slot[1] = n + 1
        self._commit((key, 16 * (n + 1)), reads, writes)

    def ld(self, out, in_, r=(), w=()):
        self.dma("sp", out, in_, r, w)

    def st(self, out, in_, r=(), w=()):
        self.dma("pool", out, in_, r, w)

    def barrier(self):
        toks = []
        for e in ["pe", "act", "dve", "pool"]:
            key = self.cur[e]
            if self.cnt[key] > 0:
                toks.append((key, self.cnt[key]))
        for q in self.dring:
            for key, n in self.dring[q][0]:
                if n > 0:
                    toks.append((key, 16 * n))
        for e in self.eng:
            for t in toks:
                self._wait(e, *t)
        self.lastw.clear()
        self.lastr.clear()

    def flush_pe(self, dummy_fn):
        pass


def host_consts(L):
    bf = ml_dtypes.bfloat16
    c = {}
    c["c_identb"] = np.eye(128, dtype=np.float32).astype(bf)
    c["c_identf"] = np.eye(128, dtype=np.float32)
    c["c_onesb"] = np.ones((128, 128), np.float32).astype(bf)
    blk = np.zeros((128, 128), np.float32)
    blk[:64, :64] = 1.0
    blk[64:, 64:] = 1.0
    c["c_blkb"] = blk.astype(bf)
    s = np.arange(128)[:, None]
    t = np.arange(128)[None, :]
    c["c_maskf"] = (s <= t).astype(np.float32)
    c["c_maskb"] = (s >= t).astype(np.float32)
    c["c_onesf"] = np.ones((128, 128), np.float32)
    k = np.arange(L)
    kaug = np.stack([np.ones(L), np.ones(L), np.ones(L), k % 128, (k // 128) * 128]).astype(np.float32)
    c["c_kaug"] = kaug.astype(bf)
    q = np.arange(L)
    r = (q % 512) % 256
    b256 = ((q % 512) // 256) * 256
    q0 = (q // 512) * 512
    qa = np.zeros((NH, 2, 5, L), np.float32)
    for h in range(NH):
        sl = SLOPES[h]
        left = np.stack([-sl * r, -sl * b256, -sl * q0, sl * np.ones(L), sl * np.ones(L)])
        qa[h, 0] = left
        qa[h, 1] = -left
    c["c_qaug"] = qa.astype(bf)
    assert np.array_equal(c["c_qaug"].astype(np.float32), qa)
    assert np.array_equal(c["c_kaug"].astype(np.float32), kaug)
    p = np.arange(128)[:, None]
    x = np.arange(512)[None, :]
    c["c_bbase"] = np.stack([-np.abs(128 * i + p - x) for i in range(4)]).astype(np.float32)
    return c


def build(L, NSEQ, debug=False, upto=9):
    NB = L // 128
    NT = L // 512
    nc = bass.Bass("TRN2", target_bir_lowering=False)
    S = Sched(nc)

    def din(name, shape, dt=F32):
        return nc.dram_tensor(name, list(shape), dt, kind="ExternalInput").ap()

    def dscr(name, shape, dt):
        return nc.dram_tensor(name, list(shape), dt, kind="ExternalOutput" if debug else "Internal").ap()

    x = din("x", [NSEQ, L, D])
    y = nc.dram_tensor("y", [NSEQ, L, D], F32, kind="ExternalOutput").ap()
    w_in = din("w_in", [D, 4096])
    w_out = din("w_out", [D, D])
    w_up = din("w_up", [D, 2 * DFF])
    w_down = din("w_down", [DFF, D])
    p_nmw = din("p_nmw", [128, 8])
    p_nfw = din("p_nfw", [128, 8])
    p_qnw = din("p_qnw", [128, 1])
    p_knw = din("p_knw", [128, 1])
    p_lam = din("p_lam", [128, 4, 64])
    p_aonw = din("p_aonw", [128, 1])
    p_ronw = din("p_ronw", [128, 1])
    p_lb = din("p_lb", [128, 2, NH, 2])
    p_cw = din("p_cw", [128, 44, 3])
    p_cb = din("p_cb", [128, 44])
    c_identb = din("c_identb", [128, 128], BF16)
    c_identf = din("c_identf", [128, 128])
    c_onesb = din("c_onesb", [128, 128], BF16)
    c_blkb = din("c_blkb", [128, 128], BF16)
    c_maskf = din("c_maskf", [128, 128])
    c_maskb = din("c_maskb", [128, 128])
    c_onesf = din("c_onesf", [128, 128])
    c_kaug = din("c_kaug", [5, L], BF16)
    c_qaug = din("c_qaug", [NH, 2, 5, L], BF16)
    c_bbase = din("c_bbase", [4, 128, 512])

    WIN = dscr("WIN", [128, 8, 4096], BF16)
    WOUT = dscr("WOUT", [128, 8, D], BF16)
    WUP = dscr("WUP", [128, 8, 2 * DFF], BF16)
    WDN = dscr("WDN", [8, 128, 22, 128], BF16)
    QT = dscr("QT", [NSEQ, NH, 128, L], BF16)
    KT = dscr("KT", [NSEQ, NH, 128, L], BF16)
    VV = dscr("VV", [NSEQ, NH, 128, NB, 128], BF16)
    RQ = dscr("RQ", [NSEQ, NH, 128, L], BF16)
    LOGF = dscr("LOGF", [NSEQ, 2, NH, 128, L], F32)
    KK = dscr("KK", [NSEQ, 2, NH, 128, L], BF16)
    RI = dscr("RI", [NSEQ, NH, 128, NB, 128], BF16)
    RG = dscr("RG", [NSEQ, NH, 128, L], BF16)
    MIXT = dscr("MIXT", [NSEQ, 8, 128, L], BF16)

    held = []

    def sb(name, shape, dt):
        cm = nc.sbuf_tensor(name, list(shape), dt)
        t = cm.__enter__()
        held.append(cm)
        return t

    class Scope:
        def __init__(self):
            self.cms = []

        def sb(self, name, shape, dt):
            cm = nc.sbuf_tensor(name, list(shape), dt)
            t = cm.__enter__()
            self.cms.append(cm)
            return t

        def ps(self, name, shape, dt=F32):
            cm = nc.psum_tensor(name, list(shape), dt)
            t = cm.__enter__()
            self.cms.append(cm)
            return t

        def close(self):
            S.barrier()
            for cm in reversed(self.cms):
                cm.__exit__(None, None, None)
            self.cms = []

    identb = sb("identb", [128, 128], BF16)
    identf = sb("identf", [128, 128], F32)
    onesb = sb("onesb", [128, 128], BF16)
    blkb = sb("blkb", [128, 128], BF16)
    maskf = sb("maskf", [128, 128], F32)
    maskb = sb("maskb", [128, 128], F32)
    onesf = sb("onesf", [128, 128], F32)
    nmw = sb("nmw", [128, 8], F32)
    nfw = sb("nfw", [128, 8], F32)
    qnw = sb("qnw", [128, 1], F32)
    knw = sb("knw", [128, 1], F32)
    lamv = sb("lamv", [128, 4, 64], F32)
    aonw = sb("aonw", [128, 1], F32)
    ronw = sb("ronw", [128, 1], F32)
    lbt = sb("lbt", [128, 2, NH, 2], F32)
    cw = sb("cw", [128, 44, 3], F32)
    cb = sb("cb", [128, 44], F32)
    lbv = sb("lbv", [128, 2, NH], F32)
    olb = sb("olb", [128, 2, NH], F32)
    nlam = sb("nlam", [128, 1], F32)
    sm = sb("sm", [128, 8], F32)

    for t, src, key in [(identb, c_identb, "identb"), (identf, c_identf, "identf"), (onesb, c_onesb, "onesb"),
                        (blkb, c_blkb, "blkb"), (maskf, c_maskf, "maskf"), (maskb, c_maskb, "maskb"),
                        (onesf, c_onesf, "onesf"), (nmw, p_nmw, "nmw"), (nfw, p_nfw, "nfw"), (qnw, p_qnw, "qnw"),
                        (knw, p_knw, "knw"), (aonw, p_aonw, "aonw"), (ronw, p_ronw, "ronw"), (cb, p_cb, "cb")]:
        S.ld(t[:], src[:, :], w=[key])
    S.ld(lamv[:], p_lam[:, :, :], w=["lamv"])
    S.ld(lbt[:], p_lb[:, :, :, :], w=["lbt"])
    S.ld(cw[:], p_cw[:, :, :], w=["cw"])

    S.dve(lambda e: e.tensor_scalar(out=qnw[:], in0=qnw[:], scalar1=0.125, scalar2=None, op0=ALU.mult), ["qnw"], ["qnw"])
    S.dve(lambda e: e.tensor_scalar(out=aonw[:], in0=aonw[:], scalar1=1.0 - LAM_INIT, scalar2=None, op0=ALU.mult), ["aonw"], ["aonw"])
    prod = sb("prod", [128, 2, 64], F32)
    S.dve(lambda e: e.tensor_tensor(out=prod[:, 0, :], in0=lamv[:, 0, :], in1=lamv[:, 1, :], op=ALU.mult), ["lamv"], ["prod"])
    S.dve(lambda e: e.tensor_tensor(out=prod[:, 1, :], in0=lamv[:, 2, :], in1=lamv[:, 3, :], op=ALU.mult), ["lamv", "prod"], ["prod"])
    S.dve(lambda e: e.tensor_reduce(out=sm[:, 0:2], in_=prod[:], axis=mybir.AxisListType.X, op=ALU.add), ["prod"], ["sm"])
    S.act(lambda e: e.activation(out=sm[:, 2:4], in_=sm[:, 0:2], func=AF.Exp), ["sm"], ["sm"])
    S.dve(lambda e: e.scalar_tensor_tensor(out=nlam[:], in0=sm[:, 3:4], scalar=-LAM_INIT, in1=sm[:, 2:3],
                                           op0=ALU.add, op1=ALU.subtract), ["sm"], ["nlam"])
    lbtmp = sb("lbtmp", [128, 2, NH], F32)
    S.dve(lambda e: e.tensor_tensor(out=lbtmp[:], in0=lbt[:, :, :, 1], in1=lbt[:, :, :, 0], op=ALU.subtract), ["lbt"], ["lbtmp"])
    S.act(lambda e: e.activation(out=lbtmp[:], in_=lbtmp[:], func=AF.Exp), ["lbtmp"], ["lbtmp"])
    S.dve(lambda e: e.tensor_scalar(out=lbtmp[:], in0=lbtmp[:], scalar1=1.0, scalar2=None, op0=ALU.add), ["lbtmp"], ["lbtmp"])
    S.dve(lambda e: e.reciprocal(out=lbv[:], in_=lbtmp[:]), ["lbtmp"], ["lbv"])
    S.dve(lambda e: e.tensor_scalar(out=olb[:], in0=lbv[:], scalar1=-1.0, scalar2=1.0, op0=ALU.mult, op1=ALU.add), ["lbv"], ["olb"])

    def rstd_from(sc, ss_ap, n, out_ap, tmp_ap, rkeys, wkeys):
        S.act(lambda e: e.activation(out=tmp_ap, in_=ss_ap, func=AF.Ln, scale=1.0 / n, bias=epsb[:, 0:1]), rkeys, wkeys)
        S.act(lambda e: e.activation(out=out_ap, in_=tmp_ap, func=AF.Exp, scale=-0.5), wkeys, wkeys)

    epsb = sb("epsb", [128, 1], F32)
    S.dve(lambda e: e.memset(epsb[:], EPS), [], ["epsb"])
    S.barrier()

    sc = Scope()
    stg = [sc.sb(f"stg{i}", [128, 2048], F32) for i in range(2)]
    stb = [sc.sb(f"stb{i}", [128, 2048], BF16) for i in range(2)]
    it = 0

    def conv_piece(src_ap, width, scale_ap, dst_ap, dst_view=None):
        nonlocal it
        i = it % 2
        it += 1
        S.ld(stg[i][:, 0:width], src_ap, w=[f"stg{i}"])
        if scale_ap is not None:
            S.dve(lambda e: e.tensor_scalar(out=stb[i][:, 0:width], in0=stg[i][:, 0:width], scalar1=scale_ap,
                                            scalar2=None, op0=ALU.mult), [f"stg{i}"], [f"stb{i}"])
        else:
            S.dve(lambda e: e.tensor_copy(out=stb[i][:, 0:width], in_=stg[i][:, 0:width]), [f"stg{i}"], [f"stb{i}"])
        src = stb[i][:, 0:width] if dst_view is None else dst_view(stb[i])
        S.st(dst_ap, src, r=[f"stb{i}"])

    for c in range(8):
        for hf in range(2):
            conv_piece(w_in[c * 128:(c + 1) * 128, hf * 2048:(hf + 1) * 2048], 2048, nmw[:, c:c + 1],
                       WIN[:, c, hf * 2048:(hf + 1) * 2048])
        conv_piece(w_out[c * 128:(c + 1) * 128, :], 1024, None, WOUT[:, c, :])
        for (o, wd) in [(0, 2048), (2048, 2048), (4096, 1536)]:
            conv_piece(w_up[c * 128:(c + 1) * 128, o:o + wd], wd, nfw[:, c:c + 1], WUP[:, c, o:o + wd])
    for j in range(22):
        conv_piece(w_down[j * 128:(j + 1) * 128, :], 1024, None,
                   WDN[:, :, j, :].rearrange("c p n -> p c n"),
                   dst_view=lambda t: t[:, 0:1024].rearrange("p (c n) -> p c n", c=8))
    sc.close()

    if upto == 0:
        return nc
    sc = Scope()
    win = sc.sb("win", [128, 8, 4096], BF16)
    for c in range(8):
        S.ld(win[:, c, :], WIN[:, c, :], w=["win"])
    xt = [sc.sb(f"xt{i}", [128, 4, D], F32) for i in range(2)]
    junk = sc.sb("junk", [128, D], BF16)
    hb = [sc.sb(f"hb{i}", [128, D], BF16) for i in range(2)]
    hT = [sc.sb(f"hT{i}", [128, 8, 512], BF16) for i in range(2)]
    ssb = sc.sb("ssb", [128, 4], F32)
    sqb = [sc.sb(f"sqb{i}", [128, 512], BF16) for i in range(2)]
    t1 = [sc.sb(f"t1{i}", [128, 512], F32) for i in range(2)]
    rs = [sc.sb(f"rs{i}", [128, 512], F32) for i in range(2)]
    ob = [sc.sb(f"ob{i}", [128, 512], BF16) for i in range(4)]
    of32 = [sc.sb(f"of{i}", [128, 512], F32) for i in range(8)]
    lf = [sc.sb(f"lf{i}", [128, 512], F32) for i in range(2)]
    tp = sc.ps("tp", [128, 8, 128], BF16)
    pq = [sc.ps(f"pq{i}", [128, 512]) for i in range(4)]
    pss = [sc.ps(f"pss{i}", [128, 512]) for i in range(2)]
    obi = 0
    gi = 0
    for s in range(0 if SKIP123 else NSEQ):
        for tt in range(NT):
            t0 = tt * 512
            xi = (s * NT + tt) % 2
            X, XK = xt[xi], f"xt{xi}"
            HT, HK = hT[xi], f"hT{xi}"
            S.ld(X[:], x[s, t0:t0 + 512, :].rearrange("(b p) f -> p b f", p=128), w=[XK])
            for b in range(4):
                hbi = b % 2
                S.act(lambda e: e.activation(out=junk[:], in_=X[:, b, :], func=AF.Square, accum_out=ssb[:, b:b + 1]),
                      [XK], ["junk", "ssb"])
                rstd_from(sc, ssb[:, b:b + 1], D, ssb[:, b:b + 1], ssb[:, b:b + 1], ["ssb"], ["ssb"])
                S.dve(lambda e: e.tensor_scalar(out=hb[hbi][:], in0=X[:, b, :], scalar1=ssb[:, b:b + 1], scalar2=None,
                                                op0=ALU.mult), [XK, "ssb"], [f"hb{hbi}"])
                for c in range(8):
                    S.pe(lambda e: e.transpose(out=tp[:, c, :], in_=hb[hbi][:, c * 128:(c + 1) * 128], identity=identb[:]),
                         [f"hb{hbi}"], ["tp"], inc=(c == 7))
                S.dve(lambda e: e.tensor_copy(out=HT[:, :, b * 128:(b + 1) * 128], in_=tp[:]), ["tp"], [HK])

            def proj_fm(col0, pi):
                for c in range(8):
                    S.pe(lambda e: e.matmul(pq[pi][:], lhsT=win[:, c, col0:col0 + 128], rhs=HT[:, c, :],
                                            start=(c == 0), stop=(c == 7)), ["win", HK], [f"pq{pi}"], inc=(c == 7))

            def proj_tm(col0, b, pi):
                for c in range(8):
                    S.pe(lambda e: e.matmul(pq[pi][:], lhsT=HT[:, c, b * 128:(b + 1) * 128], rhs=win[:, c, col0:col0 + 512],
                                            start=(c == 0), stop=(c == 7)), ["win", HK], [f"pq{pi}"], inc=(c == 7))

            for g in range(8):
                isk = g >= 4
                h = g % 4
                pi = gi % 4
                si = gi % 2
                gi += 1
                proj_fm((512 if isk else 0) + h * 128, pi)
                S.act(lambda e: e.activation(out=sqb[si][:], in_=pq[pi][:], func=AF.Square), [f"pq{pi}"], [f"sqb{si}"])
                S.pe(lambda e: e.matmul(pss[si][:], lhsT=blkb[:], rhs=sqb[si][:], start=True, stop=True),
                     [f"sqb{si}", "blkb"], [f"pss{si}"])
                rstd_from(sc, pss[si][:], 64, rs[si][:], t1[si][:], [f"pss{si}"], [f"t1{si}", f"rs{si}"])
                oi = obi % 4
                obi += 1
                wcol = knw if isk else qnw
                S.dve(lambda e: e.scalar_tensor_tensor(out=ob[oi][:], in0=pq[pi][:], scalar=wcol[:, 0:1], in1=rs[si][:],
                                                       op0=ALU.mult, op1=ALU.mult),
                      [f"pq{pi}", f"rs{si}", "qnw", "knw"], [f"ob{oi}"])
                dst = (KT if isk else QT)[s, h, :, t0:t0 + 512]
                S.st(dst, ob[oi][:], r=[f"ob{oi}"])
            for (col0, DST) in [(1024, VV), (3072, RI)]:
                for b in range(4):
                    pi = gi % 4
                    gi += 1
                    proj_tm(col0, b, pi)
                    oi = obi % 4
                    obi += 1
                    S.dve(lambda e: e.tensor_copy(out=ob[oi][:], in_=pq[pi][:]), [f"pq{pi}"], [f"ob{oi}"])
                    S.st(DST[s, :, :, tt * 4 + b, :].rearrange("h p v -> p h v"),
                         ob[oi][:].rearrange("p (h v) -> p h v", h=4), r=[f"ob{oi}"])
            for (col0, DST) in [(1536, RQ), (3584, RG)]:
                for h in range(4):
                    pi = gi % 4
                    gi += 1
                    proj_fm(col0 + h * 128, pi)
                    oi = obi % 4
                    obi += 1
                    S.act(lambda e: e.activation(out=ob[oi][:], in_=pq[pi][:], func=AF.Silu), [f"pq{pi}"], [f"ob{oi}"])
                    S.st(DST[s, h, :, t0:t0 + 512], ob[oi][:], r=[f"ob{oi}"])
            for d in range(2):
                for h in range(4):
                    pi = gi % 4
                    gi += 1
                    fi = d * 4 + h
                    proj_fm(2048 + d * 512 + h * 128, pi)
                    S.act(lambda e: e.activation(out=of32[fi][:], in_=pq[pi][:], func=AF.Sigmoid), [f"pq{pi}"], [f"of{fi}"])
                    S.dve(lambda e: e.tensor_scalar(out=of32[fi][:], in0=of32[fi][:], scalar1=olb[:, d, h:h + 1],
                                                    scalar2=lbv[:, d, h:h + 1], op0=ALU.mult, op1=ALU.add),
                          [f"of{fi}", "olb", "lbv"], [f"of{fi}"])
                    oi = obi % 4
                    obi += 1
                    S.pool(lambda e: e.tensor_scalar(out=ob[oi][:], in0=of32[fi][:], scalar1=-1.0, scalar2=1.0,
                                                     op0=ALU.mult, op1=ALU.add), [f"of{fi}"], [f"ob{oi}"])
                    S.st(KK[s, d, h, :, t0:t0 + 512], ob[oi][:], r=[f"ob{oi}"])
            for d in range(2):
                for h in range(4):
                    fi = d * 4 + h
                    li = fi % 2
                    S.act(lambda e: e.activation(out=lf[li][:], in_=of32[fi][:], func=AF.Ln), [f"of{fi}"], [f"lf{li}"])
                    S.st(LOGF[s, d, h, :, t0:t0 + 512], lf[li][:], r=[f"lf{li}"])
    sc.close()

    if upto == 1:
        return nc
    sc = Scope()
    Kt = [[sc.sb(f"K{p}{j}", [128, L], BF16) for j in range(2)] for p in range(2)]
    Vt = [sc.sb(f"V{p}", [128, NB, 128], BF16) for p in range(2)]
    QS = [[[sc.sb(f"QS{p}{v}{j}", [128, 512], BF16) for j in range(2)] for v in range(3)] for p in range(2)]
    bbase = sc.sb("bbase", [128, 4, 512], F32)
    sd = sc.sb("sd", [128, 1024], F32)
    pt = [sc.sb(f"pt{i}", [128, 1024], BF16) for i in range(4)]
    zacc = [[sc.sb(f"zacc{p}{j}", [128, 512], F32) for j in range(2)] for p in range(2)]
    zh = [sc.sb(f"zh{j}", [128, 512], BF16) for j in range(2)]
    zl = [sc.sb(f"zl{j}", [128, 512], BF16) for j in range(2)]
    ev = [[sc.sb(f"ev{p}{i}", [128, 512], F32) for i in range(4)] for p in range(2)]
    ao = sc.sb("ao", [128, 512], F32)
    asq = sc.sb("asq", [128, 512], BF16)
    at1 = sc.sb("at1", [128, 512], F32)
    ars = sc.sb("ars", [128, 512], F32)
    mo = [sc.sb(f"mo{i}", [128, 512], BF16) for i in range(2)]
    scp = [sc.ps(f"scp{i}", [128, 1024]) for i in range(3)]
    acc = [sc.ps(f"acc{i}", [128, 512]) for i in range(2)]
    for i in range(4):
        S.ld(bbase[:, i, :], c_bbase[i, :, :], w=["bbase"])
    for p in range(2):
        for j in range(2):
            S.pool(lambda e: e.memset(Kt[p][j][:], 0.0), [], [f"K{p}"])
            for v in range(3):
                S.pool(lambda e: e.memset(QS[p][v][j][:], 0.0), [], [f"QS{p}"])
    pair = 0
    qkc = 0
    hcount = 0
    qcount = 0
    evc = 0
    deferred = []

    def epilogue_tail(s, h, q0, E4):
        nonlocal pair
        ek = [f"ev{E4}{i}" for i in range(4)]
        EV = ev[E4]
        pz = pair % 3
        pair += 1
        for j in range(2):
            eng = S.dve if j == 0 else S.pool
            eng(lambda e: e.tensor_copy(out=zh[j][:], in_=zacc[E4][j][:]), [f"zacc{E4}{j}"], [f"zh{j}"])
            eng(lambda e: e.tensor_tensor(out=zl[j][:], in0=zacc[E4][j][:], in1=zh[j][:], op=ALU.subtract),
                [f"zacc{E4}{j}", f"zh{j}"], [f"zl{j}"])
            S.pe(lambda e: e.matmul(scp[pz][:, j * 512:(j + 1) * 512], lhsT=onesb[:], rhs=zh[j][:], start=True, stop=False),
                 ["onesb", f"zh{j}"], [f"scp{pz}"], inc=False)
            S.pe(lambda e: e.matmul(scp[pz][:, j * 512:(j + 1) * 512], lhsT=onesb[:], rhs=zl[j][:], start=False, stop=True),
                 ["onesb", f"zl{j}"], [f"scp{pz}"])
        for j in range(2):
            S.dve(lambda e: e.reciprocal(out=EV[2 + j][:], in_=scp[pz][:, j * 512:(j + 1) * 512]), [f"scp{pz}"], [ek[2 + j]])
            S.dve(lambda e: e.tensor_tensor(out=EV[j][:], in0=EV[j][:], in1=EV[2 + j][:], op=ALU.mult),
                  [ek[j], ek[2 + j]], [ek[j]])
        S.dve(lambda e: e.scalar_tensor_tensor(out=ao[:], in0=EV[1][:], scalar=nlam[:, 0:1], in1=EV[0][:],
                                               op0=ALU.mult, op1=ALU.add), [ek[0], ek[1], "nlam"], ["ao"])
        S.act(lambda e: e.activation(out=asq[:], in_=ao[:], func=AF.Square), ["ao"], ["asq"])
        pi = pair % 3
        pair += 1
        S.pe(lambda e: e.matmul(scp[pi][:, 0:512], lhsT=onesb[:], rhs=asq[:], start=True, stop=True),
             ["asq", "onesb"], [f"scp{pi}"])
        rstd_from(sc, scp[pi][:, 0:512], 128, ars[:], at1[:], [f"scp{pi}"], ["at1", "ars"])
        mi = (q0 // 512) % 2
        S.dve(lambda e: e.scalar_tensor_tensor(out=mo[mi][:], in0=ao[:], scalar=aonw[:, 0:1], in1=ars[:],
                                               op0=ALU.mult, op1=ALU.mult), ["ao", "ars", "aonw"], [f"mo{mi}"])
        S.st(MIXT[s, h, :, q0:q0 + 512], mo[mi][:], r=[f"mo{mi}"])

    for s in range(0 if SKIP123 else NSEQ):
        for h in range(NH):
            hp = hcount % 2
            hcount += 1
            sl = SLOPES[h]
            for j in range(2):
                S.ld(Kt[hp][j][0:64, :], KT[s, h, j * 64:(j + 1) * 64, :], w=[f"K{hp}"])
                S.ld(Kt[hp][j][64:69, :], c_kaug[:, :], w=[f"K{hp}"])
            S.ld(Vt[hp][:], VV[s, h, :, :, :], w=[f"V{hp}"])
            for qt in range(NT):
                q0 = qt * 512
                qp = qcount % 2
                qcount += 1
                for v in range(3):
                    for j in range(2):
                        S.ld(QS[qp][v][j][0:64, :], QT[s, h, j * 64:(j + 1) * 64, q0:q0 + 512], w=[f"QS{qp}"])
                        if v < 2:
                            S.ld(QS[qp][v][j][64:69, :], c_qaug[h, v, :, q0:q0 + 512], w=[f"QS{qp}"])
                kbs = []
                for kb in range(NB):
                    k0 = kb * 128
                    if k0 + 127 < q0:
                        dmin = q0 - (k0 + 127)
                        typ = 0
                    elif k0 >= q0 + 512:
                        dmin = k0 - (q0 + 511)
                        typ = 1
                    else:
                        dmin = 0
                        typ = 2
                    if dmin * sl > SKIP_T:
                        continue
                    kbs.append((kb, typ))
                nk = len(kbs)
                slots = []

                def emit_qk(idx):
                    nonlocal pair
                    kb, typ = kbs[idx]
                    nonlocal qkc
                    pi = pair % 3
                    pair += 1
                    pti = qkc % 4
                    qkc += 1
                    slots.append(pti)
                    k0 = kb * 128
                    for j in range(2):
                        S.pe(lambda e: e.matmul(scp[pi][:, j * 512:(j + 1) * 512], lhsT=Kt[hp][j][:, k0:k0 + 128],
                                                rhs=QS[qp][typ][j][:], start=True, stop=True),
                             [f"K{hp}", f"QS{qp}"], [f"scp{pi}"], inc=(j == 1))
                    if typ == 2:
                        i = kb - 4 * qt
                        for j in range(2):
                            S.dve(lambda e: e.scalar_tensor_tensor(out=sd[:, j * 512:(j + 1) * 512], in0=bbase[:, i, :], scalar=sl,
                                                                   in1=scp[pi][:, j * 512:(j + 1) * 512], op0=ALU.mult, op1=ALU.add),
                                  ["bbase", f"scp{pi}"], ["sd"])
                        S.act(lambda e: e.activation(out=pt[pti][:], in_=sd[:], func=AF.Exp), ["sd"], [f"pt{pti}"])
                    else:
                        S.act(lambda e: e.activation(out=pt[pti][:], in_=scp[pi][:], func=AF.Exp), [f"scp{pi}"], [f"pt{pti}"])

                def emit_pv(idx):
                    kb, typ = kbs[idx]
                    pti = slots[idx]
                    first = idx == 0
                    last = idx == nk - 1
                    for j in range(2):
                        S.pe(lambda e: e.matmul(acc[j][:], lhsT=Vt[hp][:, kb, :], rhs=pt[pti][:, j * 512:(j + 1) * 512],
                                                start=first, stop=last), [f"V{hp}", f"pt{pti}"], [f"acc{j}"], inc=(j == 1))
                    for j in range(2):
                        eng = S.dve
                        if first:
                            eng(lambda e: e.tensor_copy(out=zacc[E4][j][:], in_=pt[pti][:, j * 512:(j + 1) * 512]),
                                [f"pt{pti}"], [f"zacc{E4}{j}"])
                        else:
                            eng(lambda e: e.tensor_tensor(out=zacc[E4][j][:], in0=zacc[E4][j][:], in1=pt[pti][:, j * 512:(j + 1) * 512],
                                                          op=ALU.add), [f"pt{pti}", f"zacc{E4}{j}"], [f"zacc{E4}{j}"])

                E4 = evc % 2
                evc += 1
                for idx in range(nk):
                    emit_qk(idx)
                    if idx >= 2:
                        emit_pv(idx - 2)
                    if idx == 4 and deferred:
                        epilogue_tail(*deferred.pop(0))
                for idx in range(max(nk - 2, 0), nk):
                    emit_pv(idx)
                while deferred:
                    epilogue_tail(*deferred.pop(0))
                for i in range(2):
                    S.dve(lambda e: e.tensor_copy(out=ev[E4][i][:], in_=acc[i][:]), [f"acc{i}"], [f"ev{E4}{i}"])
                deferred.append((s, h, q0, E4))
    while deferred:
        epilogue_tail(*deferred.pop(0))
    sc.close()

    if upto == 2:
        return nc
    sc = Scope()
    rq = sc.sb("rq", [128, L], BF16)
    rg = sc.sb("rg", [128, L], BF16)
    ri = sc.sb("ri", [128, NB, 128], BF16)
    lgf = sc.sb("lgf", [128, L], F32)
    kkt = sc.sb("kkt", [128, L], BF16)
    ofw = sc.sb("ofw", [128, L], F32)
    Pt = [sc.sb(f"Pt{i}", [128, 128], F32) for i in range(2)]
    Gt = sc.sb("Gt", [128, 128], F32)
    rsm = [sc.sb(f"rsm{i}", [128, 4], F32) for i in range(2)]
    E = [[sc.sb(f"E{i}{k}", [128, 128], F32) for k in range(3)] for i in range(2)]
    qg = [sc.sb(f"qg{i}", [128, 128], BF16) for i in range(2)]
    kg = [sc.sb(f"kg{i}", [128, 128], BF16) for i in range(2)]
    qs = [sc.sb(f"qs{i}", [128, 128], BF16) for i in range(2)]
    ATs = [sc.sb(f"AT{i}", [128, 128], BF16) for i in range(2)]
    Acl = [sc.sb(f"Acl{i}", [128, 128], F32) for i in range(2)]
    kgt = [sc.sb(f"kgt{i}", [128, 128], BF16) for i in range(2)]
    Sf = sc.sb("Sf", [128, 128], F32)
    Sb = [sc.sb(f"Sb{i}", [128, 128], BF16) for i in range(2)]
    wtmp = sc.sb("wtmp", [128, 128], F32)
    osum = sc.sb("osum", [128, 512], F32)
    osq = sc.sb("osq", [128, 512], BF16)
    ot1 = sc.sb("ot1", [128, 512], F32)
    ors = sc.sb("ors", [128, 512], F32)
    omo = [sc.sb(f"omo{i}", [128, 512], BF16) for i in range(2)]
    pA = [sc.ps(f"pA{i}", [128, 512])[:, 0:128] for i in range(2)]
    pT = [sc.ps(f"pT{i}", [128, 1024], BF16)[:, 0:128] for i in range(2)]
    pO = [sc.ps(f"pO{i}", [128, 512])[:, 0:128] for i in range(2)]
    pW = sc.ps("pW", [128, 512])[:, 0:128]
    pN = sc.ps("pN", [128, 512])
    step = 0
    ocount = 0
    PIECE = 2048 if L >= 2048 else L
    for s in range(0 if SKIP123 else NSEQ):
        for h in range(NH):
            for c0 in range(0, L, PIECE):
                S.ld(rq[:, c0:c0 + PIECE], RQ[s, h, :, c0:c0 + PIECE], w=["rq"])
                S.ld(rg[:, c0:c0 + PIECE], RG[s, h, :, c0:c0 + PIECE], w=["rg"])
            S.ld(ri[:], RI[s, h, :, :, :], w=["ri"])
            for d in range(2):
                for c0 in range(0, L, PIECE):
                    S.ld(lgf[:, c0:c0 + PIECE], LOGF[s, d, h, :, c0:c0 + PIECE], w=["lgf"])
                    S.ld(kkt[:, c0:c0 + PIECE], KK[s, d, h, :, c0:c0 + PIECE], w=["kkt"])
                mask = maskf if d == 0 else maskb
                order = list(range(NB)) if d == 0 else list(range(NB - 1, -1, -1))
                par = {}

                def prep(ci):
                    nonlocal step
                    n = order[ci]
                    c0 = n * 128
                    i2 = step % 2
                    step += 1
                    par[ci] = i2
                    P, PK = Pt[i2], f"Pt{i2}"
                    sl_ = slice(c0, c0 + 128)
                    if d == 0:
                        S.dve(lambda e: e.tensor_tensor_scan(out=P[:], data0=onesf[:], data1=lgf[:, sl_], initial=0.0,
                                                             op0=ALU.mult, op1=ALU.add), ["lgf", "onesf"], [PK])
                        lastc = 127
                    else:
                        S.dve(lambda e: e.tensor_tensor_scan(out=Gt[:], data0=onesf[:], data1=lgf[:, sl_], initial=0.0,
                                                             op0=ALU.mult, op1=ALU.add), ["lgf", "onesf"], ["Gt"])
                        S.pool(lambda e: e.tensor_tensor(out=P[:], in0=lgf[:, sl_], in1=Gt[:], op=ALU.subtract), ["lgf", "Gt"], [PK])
                        S.dve(lambda e: e.tensor_scalar(out=P[:], in0=P[:], scalar1=Gt[:, 127:128], scalar2=None, op0=ALU.add),
                              [PK, "Gt"], [PK])
                        lastc = 0
                    R, RK = rsm[i2], f"rsm{i2}"
                    S.dve(lambda e: e.tensor_scalar(out=R[:, 0:1], in0=P[:, 63:64], scalar1=-1.0, scalar2=None, op0=ALU.mult), [PK], [RK])
                    S.act(lambda e: e.activation(out=E[i2][0][:], in_=P[:], func=AF.Exp, bias=R[:, 0:1]), [PK, RK], [f"E{i2}0"])
                    S.act(lambda e: e.activation(out=E[i2][1][:], in_=P[:], func=AF.Exp, scale=-1.0, bias=P[:, 63:64]), [PK], [f"E{i2}1"])
                    S.act(lambda e: e.activation(out=E[i2][2][:], in_=P[:], func=AF.Exp), [PK], [f"E{i2}2"])
                    S.act(lambda e: e.activation(out=R[:, 1:2], in_=P[:, lastc:lastc + 1], func=AF.Exp), [PK, RK], [RK])
                    S.act(lambda e: e.activation(out=R[:, 2:3], in_=P[:, lastc:lastc + 1], func=AF.Exp, bias=R[:, 0:1]), [PK, RK], [RK])
                    S.pool(lambda e: e.tensor_tensor(out=qg[i2][:], in0=rq[:, sl_], in1=E[i2][0][:], op=ALU.mult), ["rq", f"E{i2}0"], [f"qg{i2}"])
                    S.dve(lambda e: e.tensor_tensor(out=kg[i2][:], in0=kkt[:, sl_], in1=E[i2][1][:], op=ALU.mult), ["kkt", f"E{i2}1"], [f"kg{i2}"])
                    S.pool(lambda e: e.tensor_tensor(out=qs[i2][:], in0=rq[:, sl_], in1=E[i2][2][:], op=ALU.mult), ["rq", f"E{i2}2"], [f"qs{i2}"])
                    S.pe(lambda e: e.matmul(pA[i2][:], lhsT=kg[i2][:], rhs=qg[i2][:], start=True, stop=True),
                         [f"kg{i2}", f"qg{i2}"], [f"pA{i2}"])
                    S.dve(lambda e: e.tensor_scalar(out=Acl[i2][:], in0=pA[i2][:], scalar1=1e30, scalar2=-1e30, op0=ALU.min, op1=ALU.max),
                          [f"pA{i2}"], [f"Acl{i2}"])
                    S.pool(lambda e: e.tensor_tensor(out=ATs[i2][:], in0=Acl[i2][:], in1=mask[:], op=ALU.mult),
                           [f"Acl{i2}", "maskf", "maskb"], [f"AT{i2}"])
                    S.pe(lambda e: e.transpose(out=pT[i2][:], in_=kg[i2][:], identity=identb[:]), [f"kg{i2}"], [f"pT{i2}"])
                    S.act(lambda e: e.copy(out=kgt[i2][:], in_=pT[i2][:]), [f"pT{i2}"], [f"kgt{i2}"])

                def chain(ci):
                    nonlocal ocount
                    n = order[ci]
                    c0 = n * 128
                    i2 = par[ci]
                    sl_ = slice(c0, c0 + 128)
                    R, RK = rsm[i2], f"rsm{i2}"
                    sbi = (ci + 1) % 2
                    S.pe(lambda e: e.matmul(pO[i2][:], lhsT=ri[:, n, :], rhs=ATs[i2][:], start=True, stop=(ci == 0)),
                         ["ri", f"AT{i2}"], [f"pO{i2}"], inc=(ci == 0))
                    if ci > 0:
                        S.pe(lambda e: e.matmul(pO[i2][:], lhsT=Sb[ci % 2][:], rhs=qs[i2][:], start=False, stop=True),
                             [f"Sb{ci % 2}", f"qs{i2}"], [f"pO{i2}"])
                    S.pe(lambda e: e.matmul(pW[:], lhsT=kgt[i2][:], rhs=ri[:, n, :], start=True, stop=True),
                         [f"kgt{i2}", "ri"], ["pW"])
                    if ci == 0:
                        S.dve(lambda e: e.tensor_scalar(out=Sf[:], in0=pW[:], scalar1=R[:, 2:3], scalar2=None, op0=ALU.mult),
                              ["pW", RK], ["Sf"])
                    else:
                        S.dve(lambda e: e.tensor_scalar(out=wtmp[:], in0=pW[:], scalar1=R[:, 2:3], scalar2=None, op0=ALU.mult),
                              ["pW", RK], ["wtmp"])
                        S.dve(lambda e: e.scalar_tensor_tensor(out=Sf[:], in0=Sf[:], scalar=R[:, 1:2], in1=wtmp[:],
                                                               op0=ALU.mult, op1=ALU.add), ["Sf", "wtmp", RK], ["Sf"])
                    S.pool(lambda e: e.tensor_copy(out=Sb[sbi][:], in_=Sf[:]), ["Sf"], [f"Sb{sbi}"])
                    if d == 0:
                        S.act(lambda e: e.copy(out=ofw[:, sl_], in_=pO[i2][:]), [f"pO{i2}"], ["ofw"])
                    else:
                        g4 = n % 4
                        S.dve(lambda e: e.tensor_tensor(out=osum[:, g4 * 128:(g4 + 1) * 128], in0=pO[i2][:], in1=ofw[:, sl_], op=ALU.add),
                              [f"pO{i2}", "ofw"], ["osum"])
                        if g4 == 0:
                            b0 = n * 128
                            S.act(lambda e: e.activation(out=osq[:], in_=osum[:], func=AF.Square), ["osum"], ["osq"])
                            S.pe(lambda e: e.matmul(pN[:], lhsT=onesb[:], rhs=osq[:], start=True, stop=True), ["osq", "onesb"], ["pN"])
                            rstd_from(sc, pN[:], 128, ors[:], ot1[:], ["pN"], ["ot1", "ors"])
                            S.dve(lambda e: e.scalar_tensor_tensor(out=ot1[:], in0=osum[:], scalar=ronw[:, 0:1], in1=ors[:],
                                                                   op0=ALU.mult, op1=ALU.mult), ["osum", "ors", "ronw", "ot1"], ["ot1"])
                            oi = ocount % 2
                            ocount += 1
                            S.pool(lambda e: e.tensor_tensor(out=omo[oi][:], in0=ot1[:], in1=rg[:, b0:b0 + 512], op=ALU.mult),
                                   ["ot1", "rg"], [f"omo{oi}"])
                            S.st(MIXT[s, 4 + h, :, b0:b0 + 512], omo[oi][:], r=[f"omo{oi}"])

                prep(0)
                for ci in range(NB):
                    if ci + 1 < NB:
                        prep(ci + 1)
                    chain(ci)
    sc.close()

    if upto == 3:
        return nc
    W = FW
    NBW = W // 128
    sc = Scope()
    wo = sc.sb("wo", [128, 8, D], BF16)
    wu = sc.sb("wu", [128, 8, 2 * DFF], BF16)
    for c in range(8):
        S.ld(wo[:, c, :], WOUT[:, c, :], w=["wo"])
        S.ld(wu[:, c, :], WUP[:, c, :], w=["wu"])
    wd = [sc.sb(f"wd{i}", [128, 22, 128], BF16) for i in range(2)]
    xq = [sc.sb(f"xq{i}", [128, NBW, 128], F32) for i in range(4)]
    mx = sc.sb("mx", [128, 8, W], BF16)
    x1T = sc.sb("x1T", [128, 8, W], F32)
    fsq = [sc.sb(f"fsq{i}", [128, W], BF16) for i in range(2)]
    ft1 = sc.sb("ft1", [128, W], F32)
    frs = sc.sb("frs", [128, W], F32)
    h2T = sc.sb("h2T", [128, 8, W], BF16)
    big = sc.sb("big", [128, 8, W], F32)
    sg = [sc.sb(f"sg{i}", [128, W], F32) for i in range(2)]
    gT = sc.sb("gT", [128, 22, W], BF16)
    yo = sc.sb("yo", [128, D], F32)
    Bk = [sc.ps(f"B{i}", [128, 512]) for i in range(6)]
    pY = sc.ps("pY", [128, 2, 512])
    bank = [(Bk[i][:, 0:W], f"B{i}") for i in range(6)] + [(pY[:, 0, 0:W], "B6"), (pY[:, 1, 0:W], "B7")]
    p1 = [bank[0][0], bank[1][0]]
    pS = bank[2][0]
    pui = 0
    for i in range(4):
        S.pool(lambda e: e.memset(xq[i][:], 0.0), [], [f"xq{i}"])
    S.pool(lambda e: e.memset(mx[:], 0.0), [], ["mx"])
    FTt = W - 2
    starts = list(range(0, L - FTt, FTt)) + [L - FTt]
    ui = 0
    wdi = 0
    xqi = 0
    for s in range(NSEQ):
        for t0 in starts:
            w0 = t0 - 1
            lo, hi = max(w0, 0), min(w0 + W, L)
            if P4SUB < 2:
                continue
            S.ld(mx[:, :, lo - w0:hi - w0], MIXT[s, :, :, lo:hi].rearrange("c p t -> p c t"), w=["mx"])
            for c in range(8):
                pi = c % 2
                xi = xqi % 4
                xqi += 1
                for b in range(NBW):
                    r0, r1 = max(w0 + b * 128, lo), min(w0 + (b + 1) * 128, hi)
                    S.ld(xq[xi][r0 - (w0 + b * 128):r1 - (w0 + b * 128), b, :], x[s, r0:r1, c * 128:(c + 1) * 128], w=[f"xq{xi}"])
                if P4SUB < 3:
                    continue
                for k in range(8):
                    S.pe(lambda e: e.matmul(p1[pi][:], lhsT=wo[:, k, c * 128:(c + 1) * 128], rhs=mx[:, k, :],
                                            start=(k == 0), stop=False), ["wo", "mx"], [f"B{pi}"], inc=False)
                for b in range(NBW):
                    S.pe(lambda e: e.matmul(p1[pi][:, b * 128:(b + 1) * 128], lhsT=xq[xi][:, b, :], rhs=identf[:],
                                            start=False, stop=(b == NBW - 1)), [f"xq{xi}", "identf"], [f"B{pi}"], inc=(b == NBW - 1))
                if P4SUB < 4:
                    continue
                S.dve(lambda e: e.tensor_copy(out=x1T[:, c, :], in_=p1[pi][:]), [f"B{pi}"], ["x1T"])
                if P4SUB == 4 and os.environ.get("P4V") == "copyonly":
                    continue
                S.act(lambda e: e.activation(out=fsq[pi][:], in_=x1T[:, c, :], func=AF.Square), ["x1T"], [f"fsq{pi}"])
                if P4SUB == 4 and os.environ.get("P4V") == "nosum":
                    continue
                S.pe(lambda e: e.matmul(pS[:], lhsT=onesb[:], rhs=fsq[pi][:], start=(c == 0), stop=(c == 7)),
                     [f"fsq{pi}", "onesb"], ["B2"], inc=(c == 7))
            if P4SUB < 5:
                continue
            rstd_from(sc, pS[:], D, frs[:], ft1[:], ["B2"], ["ft1", "frs"])
            for c in range(8):
                eng = S.dve if c % 2 == 0 else S.pool
                eng(lambda e: e.tensor_tensor(out=h2T[:, c, :], in0=x1T[:, c, :], in1=frs[:], op=ALU.mult), ["x1T", "frs"], ["h2T"])
            if w0 < 0:
                S.dve(lambda e: e.memset(h2T[:, :, 0:1], 0.0), ["h2T"], ["h2T"])
            if w0 + W > L:
                S.dve(lambda e: e.memset(h2T[:, :, W - 1:W], 0.0), ["h2T"], ["h2T"])
            if P4STAGE < 1:
                continue
            a0, a1 = 1, W - 1
            for j in range(22):
                u2 = ui % 2
                ui += 1
                pis = []
                for k2, uc in enumerate([j, 22 + j]):
                    PU, PK = bank[pui % 8]
                    pui += 1
                    pis.append((PU, PK))
                    for k in range(8):
                        S.pe(lambda e: e.matmul(PU, lhsT=wu[:, k, uc * 128:(uc + 1) * 128], rhs=h2T[:, k, :],
                                                start=(k == 0), stop=(k == 7)), ["wu", "h2T"], [PK], inc=(k == 7))
                for k2, uc in enumerate([j, 22 + j]):
                    PU, PK = pis[k2]
                    cs = 4 + u2 * 2 + k2
                    S.act(lambda e: e.activation(out=big[:, cs, a0:a1], in_=PU[:, a0 - 1:a1 - 1], func=AF.Identity,
                                                 scale=cw[:, uc, 0:1], bias=cb[:, uc:uc + 1]), [PK, "cw", "cb"], [f"big{cs}"])
                for tap in (1, 2):
                    for k2, uc in enumerate([j, 22 + j]):
                        PU, PK = pis[k2]
                        cs = 4 + u2 * 2 + k2
                        S.dve(lambda e: e.scalar_tensor_tensor(out=big[:, cs, a0:a1], in0=PU[:, a0 - 1 + tap:a1 - 1 + tap],
                                                               scalar=cw[:, uc, tap:tap + 1], in1=big[:, cs, a0:a1],
                                                               op0=ALU.mult, op1=ALU.add), [PK, f"big{cs}", "cw"], [f"big{cs}"])
                cg, cu = 4 + u2 * 2, 4 + u2 * 2 + 1
                S.act(lambda e: e.activation(out=sg[u2][:, 1:W - 1], in_=big[:, cg, 1:W - 1], func=AF.Silu), [f"big{cg}"], [f"sg{u2}"])
                S.dve(lambda e: e.tensor_tensor(out=gT[:, j, 0:FTt], in0=sg[u2][:, 1:W - 1], in1=big[:, cu, 1:W - 1], op=ALU.mult),
                      [f"sg{u2}", f"big{cu}"], ["gT"])
            if P4STAGE < 2:
                continue
            for c in range(8):
                wi = wdi % 2
                wdi += 1
                S.ld(wd[wi][:], WDN[c, :, :, :], w=[f"wd{wi}"])
                PU, PK = bank[pui % 6]
                pui += 1
                for j in range(22):
                    S.pe(lambda e: e.matmul(PU[:, 0:FTt], lhsT=wd[wi][:, j, :], rhs=gT[:, j, 0:FTt], start=(j == 0), stop=(j == 21)),
                         [f"wd{wi}", "gT"], [PK], inc=(j == 21))
                S.dve(lambda e: e.tensor_tensor(out=big[:, c, 0:FTt], in0=PU[:, 0:FTt], in1=x1T[:, c, 1:W - 1], op=ALU.add),
                      [PK, "x1T"], [f"big{c}"])
            if P4STAGE < 3:
                continue
            for b in range(NBW):
                nb = 128 if b < NBW - 1 else FTt - 128 * (NBW - 1)
                for c in range(8):
                    S.pe(lambda e: e.transpose(out=pY[0:nb, c // 4, (c % 4) * 128:(c % 4 + 1) * 128], in_=big[:, c, b * 128:b * 128 + nb],
                                               identity=identf[:]), [f"big{c}", "identf"], [f"B{6 + c // 4}"], inc=(c % 4 == 3))
                S.act(lambda e: e.copy(out=yo[0:nb, 0:512], in_=pY[0:nb, 0, :]), ["B6"], ["yo"])
                S.dve(lambda e: e.tensor_copy(out=yo[0:nb, 512:1024], in_=pY[0:nb, 1, :]), ["B7"], ["yo"])
                S.st(y[s, t0 + b * 128:t0 + b * 128 + nb, :], yo[0:nb, :], r=["yo"])
    sc.close()
    return nc


def host_params(inp):
    f = np.float32
    g = lambda k: np.asarray(inp[k], dtype=f)
    P = {}
    P["w_in"] = np.ascontiguousarray(g("w_in")[0])
    P["w_out"] = np.ascontiguousarray(g("w_out")[0])
    P["w_up"] = np.ascontiguousarray(g("w_up")[0])
    P["w_down"] = np.ascontiguousarray(g("w_down")[0])
    P["p_nmw"] = np.ascontiguousarray(g("norm_mix_w")[0].reshape(8, 128).T)
    P["p_nfw"] = np.ascontiguousarray(g("norm_ffn_w")[0].reshape(8, 128).T)
    P["p_qnw"] = np.ascontiguousarray(np.tile(g("q_norm_w")[0], 2).reshape(128, 1))
    P["p_knw"] = np.ascontiguousarray(np.tile(g("k_norm_w")[0], 2).reshape(128, 1))
    lam = np.stack([g("lambda_q1")[0], g("lambda_k1")[0], g("lambda_q2")[0], g("lambda_k2")[0]])
    P["p_lam"] = np.ascontiguousarray(np.broadcast_to(lam[None], (128, 4, 64)))
    P["p_aonw"] = np.ascontiguousarray(g("attn_out_norm_w")[0].reshape(128, 1))
    P["p_ronw"] = np.ascontiguousarray(g("rec_out_norm_w")[0].reshape(128, 1))
    lb = np.stack([g("lb_fwd"), g("lb_bwd")])
    lb = lb.reshape(2, 2, NH, 128)
    P["p_lb"] = np.ascontiguousarray(lb.transpose(3, 0, 2, 1))
    P["p_cw"] = np.ascontiguousarray(g("conv_w")[0].reshape(3, 44, 128).transpose(2, 1, 0))
    P["p_cb"] = np.ascontiguousarray(g("conv_b")[0].reshape(44, 128).T)
    return P


_CACHE = {}


def kernel(**inputs):
    xp = np.asarray(inputs["x_prompt"], dtype=np.float32)
    xs = np.asarray(inputs["x_sample"], dtype=np.float32)
    L = xs.shape[1]
    ncores = 8
    NSEQ = 2
    P = host_params(inputs)
    P.update(host_consts(L))
    in_maps = []
    for c in range(ncores):
        m = dict(P)
        m["x"] = np.ascontiguousarray(np.stack([xs[c], xp[c % 2]]))
        in_maps.append(m)
    key = (L, NSEQ)
    if key not in _CACHE:
        _CACHE[key] = build(L, NSEQ)
    nc = _CACHE[key]
    res = run_bass_kernel_spmd(nc, in_maps, core_ids=list(range(ncores)))
    y_sample = np.stack([np.asarray(res.results[c]["y"][0], dtype=np.float32) for c in range(ncores)])
    y_prompt = np.stack([np.asarray(res.results[c]["y"][1], dtype=np.float32) for c in range(2)])
    return (y_prompt, y_sample)
```

```python
import os
import numpy as np
import ml_dtypes
import concourse.bass as bass
import concourse.mybir as mybir
from concourse.bass_utils import run_bass_kernel_spmd

F32 = mybir.dt.float32
BF16 = mybir.dt.bfloat16
AF = mybir.ActivationFunctionType
ALU = mybir.AluOpType

D = 1024
DFF = 2816
NH = 4
EPS = 1e-6
SLOPES = [2.0 ** (-2 * (i + 1)) for i in range(NH)]
LAM_INIT = 0.8 - 0.6 * 1.0
SKIP_T = 50.0
P4STAGE = int(os.environ.get('P4STAGE', '9'))
P4SUB = int(os.environ.get('P4SUB', '9'))
SKIP123 = int(os.environ.get('SKIP123', '0'))
FW = 512


class Sched:
    EPOCH = int(os.environ.get("EPOCH", "12000"))

    def __init__(self, nc, ndma=10):
        self.nc = nc
        self.eng = {"pe": nc.tensor, "act": nc.scalar, "dve": nc.vector, "pool": nc.gpsimd, "sp": nc.sync}
        self.sem = {}
        self.cnt = {}
        self.cur = {}
        self.nsem = 0
        self.waited = {k: {} for k in self.eng}
        for k in ["pe", "act", "dve", "pool"]:
            self._new_epoch(k)
        self.dring = {}
        for q in ["sp", "pool"]:
            self.dring[q] = [[self._new_dsem(q) for _ in range(ndma)], 0]
        self.lastw = {}
        self.lastr = {}
        self.pending = {k: [] for k in self.eng}

    def _alloc(self, name):
        cm = self.nc.semaphore(name)
        s = cm.__enter__()
        self.nsem += 1
        return s

    def _new_epoch(self, e):
        key = f"{e}#{self.nsem}"
        self.sem[key] = self._alloc("s_" + key.replace("#", "_"))
        self.cur[e] = key
        self.cnt[key] = 0

    def _new_dsem(self, q):
        key = f"d{q}#{self.nsem}"
        self.sem[key] = self._alloc("s_" + key.replace("#", "_"))
        return [key, 0]

    def _wait(self, e, semkey, val):
        if val <= 0:
            return
        w = self.waited[e]
        if w.get(semkey, 0) >= val:
            return
        self.eng[e].wait_ge(self.sem[semkey], val)
        w[semkey] = val

    def _deps(self, e, reads, writes):
        skip = self.cur[e].split("#")[0] if e == "pe" else None
        def need(tok):
            if skip is not None and tok[0].split("#")[0] == skip:
                return
            self._wait(e, tok[0], tok[1])
        for b in reads:
            if b in self.lastw:
                need(self.lastw[b])
        for b in writes:
            if b in self.lastw:
                need(self.lastw[b])
            for r in self.lastr.get(b, ()):
                need(r)

    def _commit(self, tok, reads, writes):
        for b in writes:
            self.lastw[b] = tok
            self.lastr[b] = []
        for b in reads:
            lst = self.lastr.setdefault(b, [])
            lst[:] = [t for t in lst if t[0] != tok[0]]
            lst.append(tok)

    def op(self, e, fn, reads=(), writes=(), inc=True):
        self._deps(e, reads, writes)
        ins = fn(self.eng[e])
        key = self.cur[e]
        if inc:
            self.cnt[key] += 1
            ins.then_inc(self.sem[key], 1)
            tok = (key, self.cnt[key])
            self._commit(tok, reads, writes)
            if self.cnt[key] >= self.EPOCH:
                self._new_epoch(e)
        else:
            tok = (key, self.cnt[key] + 1)
            self._commit(tok, reads, writes)
        return ins

    def pe(self, fn, r=(), w=(), inc=True):
        return self.op("pe", fn, r, w, inc)

    def act(self, fn, r=(), w=()):
        return self.op("act", fn, r, w)

    def dve(self, fn, r=(), w=()):
        return self.op("dve", fn, r, w)

    def pool(self, fn, r=(), w=()):
        return self.op("pool", fn, r, w)

    def dma(self, q, out, in_, reads=(), writes=()):
        ring, pos = self.dring[q]
        slot = ring[pos]
        if slot[1] >= int(os.environ.get("DROLL", "900")):
            self._wait(q, slot[0], 16 * slot[1])
            ring[pos] = slot = self._new_dsem(q)
        self.dring[q][1] = (pos + 1) % len(ring)
        key, n = slot
        self._wait(q, key, 16 * n)
        self._deps(q, reads, writes)
        self.eng[q].dma_start(out=out, in_=in_).then_inc(self.sem[key], 16)
        slot[1] = n + 1
        self._commit((key, 16 * (n + 1)), reads, writes)

    def ld(self, out, in_, r=(), w=()):
        self.dma("sp", out, in_, r, w)

    def st(self, out, in_, r=(), w=()):
        self.dma("pool", out, in_, r, w)

    def barrier(self):
        toks = []
        for e in ["pe", "act", "dve", "pool"]:
            key = self.cur[e]
            if self.cnt[key] > 0:
                toks.append((key, self.cnt[key]))
        for q in self.dring:
            for key, n in self.dring[q][0]:
                if n > 0:
                    toks.append((key, 16 * n))
        for e in self.eng:
            for t in toks:
                self._wait(e, *t)
        self.lastw.clear()
        self.lastr.clear()

    def flush_pe(self, dummy_fn):
        pass


def host_consts(L):
    bf = ml_dtypes.bfloat16
    c = {}
    c["c_identb"] = np.eye(128, dtype=np.float32).astype(bf)
    c["c_identf"] = np.eye(128, dtype=np.float32)
    c["c_onesb"] = np.ones((128, 128), np.float32).astype(bf)
    blk = np.zeros((128, 128), np.float32)
    blk[:64, :64] = 1.0
    blk[64:, 64:] = 1.0
    c["c_blkb"] = blk.astype(bf)
    s = np.arange(128)[:, None]
    t = np.arange(128)[None, :]
    c["c_maskf"] = (s <= t).astype(np.float32)
    c["c_maskb"] = (s >= t).astype(np.float32)
    c["c_onesf"] = np.ones((128, 128), np.float32)
    k = np.arange(L)
    kaug = np.stack([np.ones(L), np.ones(L), np.ones(L), k % 128, (k // 128) * 128]).astype(np.float32)
    c["c_kaug"] = kaug.astype(bf)
    q = np.arange(L)
    r = (q % 512) % 256
    b256 = ((q % 512) // 256) * 256
    q0 = (q // 512) * 512
    qa = np.zeros((NH, 2, 5, L), np.float32)
    for h in range(NH):
        sl = SLOPES[h]
        left = np.stack([-sl * r, -sl * b256, -sl * q0, sl * np.ones(L), sl * np.ones(L)])
        qa[h, 0] = left
        qa[h, 1] = -left
    c["c_qaug"] = qa.astype(bf)
    assert np.array_equal(c["c_qaug"].astype(np.float32), qa)
    assert np.array_equal(c["c_kaug"].astype(np.float32), kaug)
    p = np.arange(128)[:, None]
    x = np.arange(512)[None, :]
    c["c_bbase"] = np.stack([-np.abs(128 * i + p - x) for i in range(4)]).astype(np.float32)
    return c


def build(L, NSEQ, debug=False, upto=9):
    NB = L // 128
    NT = L // 512
    nc = bass.Bass("TRN2", target_bir_lowering=False)
    S = Sched(nc)

    def din(name, shape, dt=F32):
        return nc.dram_tensor(name, list(shape), dt, kind="ExternalInput").ap()

    def dscr(name, shape, dt):
        return nc.dram_tensor(name, list(shape), dt, kind="ExternalOutput" if debug else "Internal").ap()

    x = din("x", [NSEQ, L, D])
    y = nc.dram_tensor("y", [NSEQ, L, D], F32, kind="ExternalOutput").ap()
    w_in = din("w_in", [D, 4096])
    w_out = din("w_out", [D, D])
    w_up = din("w_up", [D, 2 * DFF])
    w_down = din("w_down", [DFF, D])
    p_nmw = din("p_nmw", [128, 8])
    p_nfw = din("p_nfw", [128, 8])
    p_qnw = din("p_qnw", [128, 1])
    p_knw = din("p_knw", [128, 1])
    p_lam = din("p_lam", [128, 4, 64])
    p_aonw = din("p_aonw", [128, 1])
    p_ronw = din("p_ronw", [128, 1])
    p_lb = din("p_lb", [128, 2, NH, 2])
    p_cw = din("p_cw", [128, 44, 3])
    p_cb = din("p_cb", [128, 44])
    c_identb = din("c_identb", [128, 128], BF16)
    c_identf = din("c_identf", [128, 128])
    c_onesb = din("c_onesb", [128, 128], BF16)
    c_blkb = din("c_blkb", [128, 128], BF16)
    c_maskf = din("c_maskf", [128, 128])
    c_maskb = din("c_maskb", [128, 128])
    c_onesf = din("c_onesf", [128, 128])
    c_kaug = din("c_kaug", [5, L], BF16)
    c_qaug = din("c_qaug", [NH, 2, 5, L], BF16)
    c_bbase = din("c_bbase", [4, 128, 512])

    WIN = dscr("WIN", [128, 8, 4096], BF16)
    WOUT = dscr("WOUT", [128, 8, D], BF16)
    WUP = dscr("WUP", [128, 8, 2 * DFF], BF16)
    WDN = dscr("WDN", [8, 128, 22, 128], BF16)
    QT = dscr("QT", [NSEQ, NH, 128, L], BF16)
    KT = dscr("KT", [NSEQ, NH, 128, L], BF16)
    VV = dscr("VV", [NSEQ, NH, 128, NB, 128], BF16)
    RQ = dscr("RQ", [NSEQ, NH, 128, L], BF16)
    LOGF = dscr("LOGF", [NSEQ, 2, NH, 128, L], F32)
    KK = dscr("KK", [NSEQ, 2, NH, 128, L], BF16)
    RI = dscr("RI", [NSEQ, NH, 128, NB, 128], BF16)
    RG = dscr("RG", [NSEQ, NH, 128, L], BF16)
    MIXT = dscr("MIXT", [NSEQ, 8, 128, L], BF16)

    held = []

    def sb(name, shape, dt):
        cm = nc.sbuf_tensor(name, list(shape), dt)
        t = cm.__enter__()
        held.append(cm)
        return t

    class Scope:
        def __init__(self):
            self.cms = []

        def sb(self, name, shape, dt):
            cm = nc.sbuf_tensor(name, list(shape), dt)
            t = cm.__enter__()
            self.cms.append(cm)
            return t

        def ps(self, name, shape, dt=F32):
            cm = nc.psum_tensor(name, list(shape), dt)
            t = cm.__enter__()
            self.cms.append(cm)
            return t

        def close(self):
            S.barrier()
            for cm in reversed(self.cms):
                cm.__exit__(None, None, None)
            self.cms = []

    identb = sb("identb", [128, 128], BF16)
    identf = sb("identf", [128, 128], F32)
    onesb = sb("onesb", [128, 128], BF16)
    blkb = sb("blkb", [128, 128], BF16)
    maskf = sb("maskf", [128, 128], F32)
    maskb = sb("maskb", [128, 128], F32)
    onesf = sb("onesf", [128, 128], F32)
    nmw = sb("nmw", [128, 8], F32)
    nfw = sb("nfw", [128, 8], F32)
    qnw = sb("qnw", [128, 1], F32)
    knw = sb("knw", [128, 1], F32)
    lamv = sb("lamv", [128, 4, 64], F32)
    aonw = sb("aonw", [128, 1], F32)
    ronw = sb("ronw", [128, 1], F32)
    lbt = sb("lbt", [128, 2, NH, 2], F32)
    cw = sb("cw", [128, 44, 3], F32)
    cb = sb("cb", [128, 44], F32)
    lbv = sb("lbv", [128, 2, NH], F32)
    olb = sb("olb", [128, 2, NH], F32)
    nlam = sb("nlam", [128, 1], F32)
    sm = sb("sm", [128, 8], F32)

    for t, src, key in [(identb, c_identb, "identb"), (identf, c_identf, "identf"), (onesb, c_onesb, "onesb"),
                        (blkb, c_blkb, "blkb"), (maskf, c_maskf, "maskf"), (maskb, c_maskb, "maskb"),
                        (onesf, c_onesf, "onesf"), (nmw, p_nmw, "nmw"), (nfw, p_nfw, "nfw"), (qnw, p_qnw, "qnw"),
                        (knw, p_knw, "knw"), (aonw, p_aonw, "aonw"), (ronw, p_ronw, "ronw"), (cb, p_cb, "cb")]:
        S.ld(t[:], src[:, :], w=[key])
    S.ld(lamv[:], p_lam[:, :, :], w=["lamv"])
    S.ld(lbt[:], p_lb[:, :, :, :], w=["lbt"])
    S.ld(cw[:], p_cw[:, :, :], w=["cw"])

    S.dve(lambda e: e.tensor_scalar(out=qnw[:], in0=qnw[:], scalar1=0.125, scalar2=None, op0=ALU.mult), ["qnw"], ["qnw"])
    S.dve(lambda e: e.tensor_scalar(out=aonw[:], in0=aonw[:], scalar1=1.0 - LAM_INIT, scalar2=None, op0=ALU.mult), ["aonw"], ["aonw"])
    prod = sb("prod", [128, 2, 64], F32)
    S.dve(lambda e: e.tensor_tensor(out=prod[:, 0, :], in0=lamv[:, 0, :], in1=lamv[:, 1, :], op=ALU.mult), ["lamv"], ["prod"])
    S.dve(lambda e: e.tensor_tensor(out=prod[:, 1, :], in0=lamv[:, 2, :], in1=lamv[:, 3, :], op=ALU.mult), ["lamv", "prod"], ["prod"])
    S.dve(lambda e: e.tensor_reduce(out=sm[:, 0:2], in_=prod[:], axis=mybir.AxisListType.X, op=ALU.add), ["prod"], ["sm"])
    S.act(lambda e: e.activation(out=sm[:, 2:4], in_=sm[:, 0:2], func=AF.Exp), ["sm"], ["sm"])
    S.dve(lambda e: e.scalar_tensor_tensor(out=nlam[:], in0=sm[:, 3:4], scalar=-LAM_INIT, in1=sm[:, 2:3],
                                           op0=ALU.add, op1=ALU.subtract), ["sm"], ["nlam"])
    lbtmp = sb("lbtmp", [128, 2, NH], F32)
    S.dve(lambda e: e.tensor_tensor(out=lbtmp[:], in0=lbt[:, :, :, 1], in1=lbt[:, :, :, 0], op=ALU.subtract), ["lbt"], ["lbtmp"])
    S.act(lambda e: e.activation(out=lbtmp[:], in_=lbtmp[:], func=AF.Exp), ["lbtmp"], ["lbtmp"])
    S.dve(lambda e: e.tensor_scalar(out=lbtmp[:], in0=lbtmp[:], scalar1=1.0, scalar2=None, op0=ALU.add), ["lbtmp"], ["lbtmp"])
    S.dve(lambda e: e.reciprocal(out=lbv[:], in_=lbtmp[:]), ["lbtmp"], ["lbv"])
    S.dve(lambda e: e.tensor_scalar(out=olb[:], in0=lbv[:], scalar1=-1.0, scalar2=1.0, op0=ALU.mult, op1=ALU.add), ["lbv"], ["olb"])

    def rstd_from(sc, ss_ap, n, out_ap, tmp_ap, rkeys, wkeys):
        S.act(lambda e: e.activation(out=tmp_ap, in_=ss_ap, func=AF.Ln, scale=1.0 / n, bias=epsb[:, 0:1]), rkeys, wkeys)
        S.act(lambda e: e.activation(out=out_ap, in_=tmp_ap, func=AF.Exp, scale=-0.5), wkeys, wkeys)

    epsb = sb("epsb", [128, 1], F32)
    S.dve(lambda e: e.memset(epsb[:], EPS), [], ["epsb"])
    S.barrier()

    sc = Scope()
    stg = [sc.sb(f"stg{i}", [128, 2048], F32) for i in range(2)]
    stb = [sc.sb(f"stb{i}", [128, 2048], BF16) for i in range(2)]
    it = 0

    def conv_piece(src_ap, width, scale_ap, dst_ap, dst_view=None):
        nonlocal it
        i = it % 2
        it += 1
        S.ld(stg[i][:, 0:width], src_ap, w=[f"stg{i}"])
        if scale_ap is not None:
            S.dve(lambda e: e.tensor_scalar(out=stb[i][:, 0:width], in0=stg[i][:, 0:width], scalar1=scale_ap,
                                            scalar2=None, op0=ALU.mult), [f"stg{i}"], [f"stb{i}"])
        else:
            S.dve(lambda e: e.tensor_copy(out=stb[i][:, 0:width], in_=stg[i][:, 0:width]), [f"stg{i}"], [f"stb{i}"])
        src = stb[i][:, 0:width] if dst_view is None else dst_view(stb[i])
        S.st(dst_ap, src, r=[f"stb{i}"])

    for c in range(8):
        for hf in range(2):
            conv_piece(w_in[c * 128:(c + 1) * 128, hf * 2048:(hf + 1) * 2048], 2048, nmw[:, c:c + 1],
                       WIN[:, c, hf * 2048:(hf + 1) * 2048])
        conv_piece(w_out[c * 128:(c + 1) * 128, :], 1024, None, WOUT[:, c, :])
        for (o, wd) in [(0, 2048), (2048, 2048), (4096, 1536)]:
            conv_piece(w_up[c * 128:(c + 1) * 128, o:o + wd], wd, nfw[:, c:c + 1], WUP[:, c, o:o + wd])
    for j in range(22):
        conv_piece(w_down[j * 128:(j + 1) * 128, :], 1024, None,
                   WDN[:, :, j, :].rearrange("c p n -> p c n"),
                   dst_view=lambda t: t[:, 0:1024].rearrange("p (c n) -> p c n", c=8))
    sc.close()

    if upto == 0:
        return nc
    sc = Scope()
    win = sc.sb("win", [128, 8, 4096], BF16)
    for c in range(8):
        S.ld(win[:, c, :], WIN[:, c, :], w=["win"])
    xt = [sc.sb(f"xt{i}", [128, 4, D], F32) for i in range(2)]
    junk = sc.sb("junk", [128, D], BF16)
    hb = [sc.sb(f"hb{i}", [128, D], BF16) for i in range(2)]
    hT = [sc.sb(f"hT{i}", [128, 8, 512], BF16) for i in range(2)]
    ssb = sc.sb("ssb", [128, 4], F32)
    sqb = [sc.sb(f"sqb{i}", [128, 512], BF16) for i in range(2)]
    t1 = [sc.sb(f"t1{i}", [128, 512], F32) for i in range(2)]
    rs = [sc.sb(f"rs{i}", [128, 512], F32) for i in range(2)]
    ob = [sc.sb(f"ob{i}", [128, 512], BF16) for i in range(4)]
    of32 = [sc.sb(f"of{i}", [128, 512], F32) for i in range(8)]
    lf = [sc.sb(f"lf{i}", [128, 512], F32) for i in range(2)]
    tp = sc.ps("tp", [128, 8, 128], BF16)
    pq = [sc.ps(f"pq{i}", [128, 512]) for i in range(4)]
    pss = [sc.ps(f"pss{i}", [128, 512]) for i in range(2)]
    obi = 0
    gi = 0
    for s in range(0 if SKIP123 else NSEQ):
        for tt in range(NT):
            t0 = tt * 512
            xi = (s * NT + tt) % 2
            X, XK = xt[xi], f"xt{xi}"
            HT, HK = hT[xi], f"hT{xi}"
            S.ld(X[:], x[s, t0:t0 + 512, :].rearrange("(b p) f -> p b f", p=128), w=[XK])
            for b in range(4):
                hbi = b % 2
                S.act(lambda e: e.activation(out=junk[:], in_=X[:, b, :], func=AF.Square, accum_out=ssb[:, b:b + 1]),
                      [XK], ["junk", "ssb"])
                rstd_from(sc, ssb[:, b:b + 1], D, ssb[:, b:b + 1], ssb[:, b:b + 1], ["ssb"], ["ssb"])
                S.dve(lambda e: e.tensor_scalar(out=hb[hbi][:], in0=X[:, b, :], scalar1=ssb[:, b:b + 1], scalar2=None,
                                                op0=ALU.mult), [XK, "ssb"], [f"hb{hbi}"])
                for c in range(8):
                    S.pe(lambda e: e.transpose(out=tp[:, c, :], in_=hb[hbi][:, c * 128:(c + 1) * 128], identity=identb[:]),
                         [f"hb{hbi}"], ["tp"], inc=(c == 7))
                S.dve(lambda e: e.tensor_copy(out=HT[:, :, b * 128:(b + 1) * 128], in_=tp[:]), ["tp"], [HK])

            def proj_fm(col0, pi):
                for c in range(8):
                    S.pe(lambda e: e.matmul(pq[pi][:], lhsT=win[:, c, col0:col0 + 128], rhs=HT[:, c, :],
                                            start=(c == 0), stop=(c == 7)), ["win", HK], [f"pq{pi}"], inc=(c == 7))

            def proj_tm(col0, b, pi):
                for c in range(8):
                    S.pe(lambda e: e.matmul(pq[pi][:], lhsT=HT[:, c, b * 128:(b + 1) * 128], rhs=win[:, c, col0:col0 + 512],
                                            start=(c == 0), stop=(c == 7)), ["win", HK], [f"pq{pi}"], inc=(c == 7))

            for g in range(8):
                isk = g >= 4
                h = g % 4
                pi = gi % 4
                si = gi % 2
                gi += 1
                proj_fm((512 if isk else 0) + h * 128, pi)
                S.act(lambda e: e.activation(out=sqb[si][:], in_=pq[pi][:], func=AF.Square), [f"pq{pi}"], [f"sqb{si}"])
                S.pe(lambda e: e.matmul(pss[si][:], lhsT=blkb[:], rhs=sqb[si][:], start=True, stop=True),
                     [f"sqb{si}", "blkb"], [f"pss{si}"])
                rstd_from(sc, pss[si][:], 64, rs[si][:], t1[si][:], [f"pss{si}"], [f"t1{si}", f"rs{si}"])
                oi = obi % 4
                obi += 1
                wcol = knw if isk else qnw
                S.dve(lambda e: e.scalar_tensor_tensor(out=ob[oi][:], in0=pq[pi][:], scalar=wcol[:, 0:1], in1=rs[si][:],
                                                       op0=ALU.mult, op1=ALU.mult),
                      [f"pq{pi}", f"rs{si}", "qnw", "knw"], [f"ob{oi}"])
                dst = (KT if isk else QT)[s, h, :, t0:t0 + 512]
                S.st(dst, ob[oi][:], r=[f"ob{oi}"])
            for (col0, DST) in [(1024, VV), (3072, RI)]:
                for b in range(4):
                    pi = gi % 4
                    gi += 1
                    proj_tm(col0, b, pi)
                    oi = obi % 4
                    obi += 1
                    S.dve(lambda e: e.tensor_copy(out=ob[oi][:], in_=pq[pi][:]), [f"pq{pi}"], [f"ob{oi}"])
                    S.st(DST[s, :, :, tt * 4 + b, :].rearrange("h p v -> p h v"),
                         ob[oi][:].rearrange("p (h v) -> p h v", h=4), r=[f"ob{oi}"])
            for (col0, DST) in [(1536, RQ), (3584, RG)]:
                for h in range(4):
                    pi = gi % 4
                    gi += 1
                    proj_fm(col0 + h * 128, pi)
                    oi = obi % 4
                    obi += 1
                    S.act(lambda e: e.activation(out=ob[oi][:], in_=pq[pi][:], func=AF.Silu), [f"pq{pi}"], [f"ob{oi}"])
                    S.st(DST[s, h, :, t0:t0 + 512], ob[oi][:], r=[f"ob{oi}"])
            for d in range(2):
                for h in range(4):
                    pi = gi % 4
                    gi += 1
                    fi = d * 4 + h
                    proj_fm(2048 + d * 512 + h * 128, pi)
                    S.act(lambda e: e.activation(out=of32[fi][:], in_=pq[pi][:], func=AF.Sigmoid), [f"pq{pi}"], [f"of{fi}"])
                    S.dve(lambda e: e.tensor_scalar(out=of32[fi][:], in0=of32[fi][:], scalar1=olb[:, d, h:h + 1],
                                                    scalar2=lbv[:, d, h:h + 1], op0=ALU.mult, op1=ALU.add),
                          [f"of{fi}", "olb", "lbv"], [f"of{fi}"])
                    oi = obi % 4
                    obi += 1
                    S.pool(lambda e: e.tensor_scalar(out=ob[oi][:], in0=of32[fi][:], scalar1=-1.0, scalar2=1.0,
                                                     op0=ALU.mult, op1=ALU.add), [f"of{fi}"], [f"ob{oi}"])
                    S.st(KK[s, d, h, :, t0:t0 + 512], ob[oi][:], r=[f"ob{oi}"])
            for d in range(2):
                for h in range(4):
                    fi = d * 4 + h
                    li = fi % 2
                    S.act(lambda e: e.activation(out=lf[li][:], in_=of32[fi][:], func=AF.Ln), [f"of{fi}"], [f"lf{li}"])
                    S.st(LOGF[s, d, h, :, t0:t0 + 512], lf[li][:], r=[f"lf{li}"])
    sc.close()

    if upto == 1:
        return nc
    sc = Scope()
    Kt = [[sc.sb(f"K{p}{j}", [128, L], BF16) for j in range(2)] for p in range(2)]
    Vt = [sc.sb(f"V{p}", [128, NB, 128], BF16) for p in range(2)]
    QS = [[[sc.sb(f"QS{p}{v}{j}", [128, 512], BF16) for j in range(2)] for v in range(3)] for p in range(2)]
    bbase = sc.sb("bbase", [128, 4, 512], F32)
    sd = sc.sb("sd", [128, 1024], F32)
    pt = [sc.sb(f"pt{i}", [128, 1024], BF16) for i in range(4)]
    zacc = [[sc.sb(f"zacc{p}{j}", [128, 512], F32) for j in range(2)] for p in range(2)]
    zh = [sc.sb(f"zh{j}", [128, 512], BF16) for j in range(2)]
    zl = [sc.sb(f"zl{j}", [128, 512], BF16) for j in range(2)]
    ev = [[sc.sb(f"ev{p}{i}", [128, 512], F32) for i in range(4)] for p in range(2)]
    ao = sc.sb("ao", [128, 512], F32)
    asq = sc.sb("asq", [128, 512], BF16)
    at1 = sc.sb("at1", [128, 512], F32)
    ars = sc.sb("ars", [128, 512], F32)
    mo = [sc.sb(f"mo{i}", [128, 512], BF16) for i in range(2)]
    scp = [sc.ps(f"scp{i}", [128, 1024]) for i in range(3)]
    acc = [sc.ps(f"acc{i}", [128, 512]) for i in range(2)]
    for i in range(4):
        S.ld(bbase[:, i, :], c_bbase[i, :, :], w=["bbase"])
    for p in range(2):
        for j in range(2):
            S.pool(lambda e: e.memset(Kt[p][j][:], 0.0), [], [f"K{p}"])
            for v in range(3):
                S.pool(lambda e: e.memset(QS[p][v][j][:], 0.0), [], [f"QS{p}"])
    pair = 0
    qkc = 0
    hcount = 0
    qcount = 0
    evc = 0
    deferred = []

    def epilogue_tail(s, h, q0, E4):
        nonlocal pair
        ek = [f"ev{E4}{i}" for i in range(4)]
        EV = ev[E4]
        pz = pair % 3
        pair += 1
        for j in range(2):
            eng = S.dve if j == 0 else S.pool
            eng(lambda e: e.tensor_copy(out=zh[j][:], in_=zacc[E4][j][:]), [f"zacc{E4}{j}"], [f"zh{j}"])
            eng(lambda e: e.tensor_tensor(out=zl[j][:], in0=zacc[E4][j][:], in1=zh[j][:], op=ALU.subtract),
                [f"zacc{E4}{j}", f"zh{j}"], [f"zl{j}"])
            S.pe(lambda e: e.matmul(scp[pz][:, j * 512:(j + 1) * 512], lhsT=onesb[:], rhs=zh[j][:], start=True, stop=False),
                 ["onesb", f"zh{j}"], [f"scp{pz}"], inc=False)
            S.pe(lambda e: e.matmul(scp[pz][:, j * 512:(j + 1) * 512], lhsT=onesb[:], rhs=zl[j][:], start=False, stop=True),
                 ["onesb", f"zl{j}"], [f"scp{pz}"])
        for j in range(2):
            S.act(lambda e: e.activation(out=EV[2 + j][:], in_=scp[pz][:, j * 512:(j + 1) * 512], func=AF.Ln), [f"scp{pz}"], [ek[2 + j]])
            S.act(lambda e: e.activation(out=EV[2 + j][:], in_=EV[2 + j][:], func=AF.Exp, scale=-1.0), [ek[2 + j]], [ek[2 + j]])
            S.dve(lambda e: e.tensor_tensor(out=EV[j][:], in0=EV[j][:], in1=EV[2 + j][:], op=ALU.mult),
                  [ek[j], ek[2 + j]], [ek[j]])
        S.dve(lambda e: e.scalar_tensor_tensor(out=ao[:], in0=EV[1][:], scalar=nlam[:, 0:1], in1=EV[0][:],
                                               op0=ALU.mult, op1=ALU.add), [ek[0], ek[1], "nlam"], ["ao"])
        S.act(lambda e: e.activation(out=asq[:], in_=ao[:], func=AF.Square), ["ao"], ["asq"])
        pi = pair % 3
        pair += 1
        S.pe(lambda e: e.matmul(scp[pi][:, 0:512], lhsT=onesb[:], rhs=asq[:], start=True, stop=True),
             ["asq", "onesb"], [f"scp{pi}"])
        rstd_from(sc, scp[pi][:, 0:512], 128, ars[:], at1[:], [f"scp{pi}"], ["at1", "ars"])
        mi = (q0 // 512) % 2
        S.dve(lambda e: e.scalar_tensor_tensor(out=mo[mi][:], in0=ao[:], scalar=aonw[:, 0:1], in1=ars[:],
                                               op0=ALU.mult, op1=ALU.mult), ["ao", "ars", "aonw"], [f"mo{mi}"])
        S.st(MIXT[s, h, :, q0:q0 + 512], mo[mi][:], r=[f"mo{mi}"])

    for s in range(0 if SKIP123 else NSEQ):
        for h in range(NH):
            hp = hcount % 2
            hcount += 1
            sl = SLOPES[h]
            for j in range(2):
                S.ld(Kt[hp][j][0:64, :], KT[s, h, j * 64:(j + 1) * 64, :], w=[f"K{hp}"])
                S.ld(Kt[hp][j][64:69, :], c_kaug[:, :], w=[f"K{hp}"])
            S.ld(Vt[hp][:], VV[s, h, :, :, :], w=[f"V{hp}"])
            for qt in range(NT):
                q0 = qt * 512
                qp = qcount % 2
                qcount += 1
                for v in range(3):
                    for j in range(2):
                        S.ld(QS[qp][v][j][0:64, :], QT[s, h, j * 64:(j + 1) * 64, q0:q0 + 512], w=[f"QS{qp}"])
                        if v < 2:
                            S.ld(QS[qp][v][j][64:69, :], c_qaug[h, v, :, q0:q0 + 512], w=[f"QS{qp}"])
                kbs = []
                for kb in range(NB):
                    k0 = kb * 128
                    if k0 + 127 < q0:
                        dmin = q0 - (k0 + 127)
                        typ = 0
                    elif k0 >= q0 + 512:
                        dmin = k0 - (q0 + 511)
                        typ = 1
                    else:
                        dmin = 0
                        typ = 2
                    if dmin * sl > SKIP_T:
                        continue
                    kbs.append((kb, typ))
                nk = len(kbs)
                slots = []

                def emit_qk(idx):
                    nonlocal pair
                    kb, typ = kbs[idx]
                    nonlocal qkc
                    pi = pair % 3
                    pair += 1
                    pti = qkc % 4
                    qkc += 1
                    slots.append(pti)
                    k0 = kb * 128
                    for j in range(2):
                        S.pe(lambda e: e.matmul(scp[pi][:, j * 512:(j + 1) * 512], lhsT=Kt[hp][j][:, k0:k0 + 128],
                                                rhs=QS[qp][typ][j][:], start=True, stop=True),
                             [f"K{hp}", f"QS{qp}"], [f"scp{pi}"], inc=(j == 1))
                    if typ == 2:
                        i = kb - 4 * qt
                        for j in range(2):
                            S.dve(lambda e: e.scalar_tensor_tensor(out=sd[:, j * 512:(j + 1) * 512], in0=bbase[:, i, :], scalar=sl,
                                                                   in1=scp[pi][:, j * 512:(j + 1) * 512], op0=ALU.mult, op1=ALU.add),
                                  ["bbase", f"scp{pi}"], ["sd"])
                        S.act(lambda e: e.activation(out=pt[pti][:], in_=sd[:], func=AF.Exp), ["sd"], [f"pt{pti}"])
                    else:
                        S.act(lambda e: e.activation(out=pt[pti][:], in_=scp[pi][:], func=AF.Exp), [f"scp{pi}"], [f"pt{pti}"])

                def emit_pv(idx):
                    kb, typ = kbs[idx]
                    pti = slots[idx]
                    first = idx == 0
                    last = idx == nk - 1
                    for j in range(2):
                        S.pe(lambda e: e.matmul(acc[j][:], lhsT=Vt[hp][:, kb, :], rhs=pt[pti][:, j * 512:(j + 1) * 512],
                                                start=first, stop=last), [f"V{hp}", f"pt{pti}"], [f"acc{j}"], inc=(j == 1))
                    for j in range(2):
                        eng = S.dve
                        if first:
                            eng(lambda e: e.tensor_copy(out=zacc[E4][j][:], in_=pt[pti][:, j * 512:(j + 1) * 512]),
                                [f"pt{pti}"], [f"zacc{E4}{j}"])
                        else:
                            eng(lambda e: e.tensor_tensor(out=zacc[E4][j][:], in0=zacc[E4][j][:], in1=pt[pti][:, j * 512:(j + 1) * 512],
                                                          op=ALU.add), [f"pt{pti}", f"zacc{E4}{j}"], [f"zacc{E4}{j}"])

                E4 = evc % 2
                evc += 1
                for idx in range(nk):
                    emit_qk(idx)
                    if idx >= 2:
                        emit_pv(idx - 2)
                    if idx == 4 and deferred:
                        epilogue_tail(*deferred.pop(0))
                for idx in range(max(nk - 2, 0), nk):
                    emit_pv(idx)
                while deferred:
                    epilogue_tail(*deferred.pop(0))
                for i in range(2):
                    S.act(lambda e: e.copy(out=ev[E4][i][:], in_=acc[i][:]), [f"acc{i}"], [f"ev{E4}{i}"])
                deferred.append((s, h, q0, E4))
    while deferred:
        epilogue_tail(*deferred.pop(0))
    sc.close()

    if upto == 2:
        return nc
    sc = Scope()
    rq = sc.sb("rq", [128, L], BF16)
    rg = sc.sb("rg", [128, L], BF16)
    ri = sc.sb("ri", [128, NB, 128], BF16)
    lgf = sc.sb("lgf", [128, L], F32)
    kkt = sc.sb("kkt", [128, L], BF16)
    ofw = sc.sb("ofw", [128, L], F32)
    Pt = [sc.sb(f"Pt{i}", [128, 128], F32) for i in range(2)]
    Gt = sc.sb("Gt", [128, 128], F32)
    rsm = [sc.sb(f"rsm{i}", [128, 4], F32) for i in range(2)]
    E = [[sc.sb(f"E{i}{k}", [128, 128], F32) for k in range(3)] for i in range(2)]
    qg = [sc.sb(f"qg{i}", [128, 128], BF16) for i in range(2)]
    kg = [sc.sb(f"kg{i}", [128, 128], BF16) for i in range(2)]
    qs = [sc.sb(f"qs{i}", [128, 128], BF16) for i in range(2)]
    ATs = [sc.sb(f"AT{i}", [128, 128], BF16) for i in range(2)]
    Acl = [sc.sb(f"Acl{i}", [128, 128], F32) for i in range(2)]
    kgt = [sc.sb(f"kgt{i}", [128, 128], BF16) for i in range(2)]
    Sf = sc.sb("Sf", [128, 128], F32)
    Sb = [sc.sb(f"Sb{i}", [128, 128], BF16) for i in range(2)]
    wtmp = sc.sb("wtmp", [128, 128], F32)
    osum = sc.sb("osum", [128, 512], F32)
    osq = sc.sb("osq", [128, 512], BF16)
    ot1 = sc.sb("ot1", [128, 512], F32)
    ors = sc.sb("ors", [128, 512], F32)
    omo = [sc.sb(f"omo{i}", [128, 512], BF16) for i in range(2)]
    pA = [sc.ps(f"pA{i}", [128, 512])[:, 0:128] for i in range(2)]
    pT = [sc.ps(f"pT{i}", [128, 1024], BF16)[:, 0:128] for i in range(2)]
    pO = [sc.ps(f"pO{i}", [128, 512])[:, 0:128] for i in range(2)]
    pW = sc.ps("pW", [128, 512])[:, 0:128]
    pN = sc.ps("pN", [128, 512])
    step = 0
    ocount = 0
    PIECE = 2048 if L >= 2048 else L
    for s in range(0 if SKIP123 else NSEQ):
        for h in range(NH):
            for c0 in range(0, L, PIECE):
                S.ld(rq[:, c0:c0 + PIECE], RQ[s, h, :, c0:c0 + PIECE], w=["rq"])
                S.ld(rg[:, c0:c0 + PIECE], RG[s, h, :, c0:c0 + PIECE], w=["rg"])
            S.ld(ri[:], RI[s, h, :, :, :], w=["ri"])
            for d in range(2):
                for c0 in range(0, L, PIECE):
                    S.ld(lgf[:, c0:c0 + PIECE], LOGF[s, d, h, :, c0:c0 + PIECE], w=["lgf"])
                    S.ld(kkt[:, c0:c0 + PIECE], KK[s, d, h, :, c0:c0 + PIECE], w=["kkt"])
                mask = maskf if d == 0 else maskb
                order = list(range(NB)) if d == 0 else list(range(NB - 1, -1, -1))
                par = {}

                def prep(ci):
                    nonlocal step
                    n = order[ci]
                    c0 = n * 128
                    i2 = step % 2
                    step += 1
                    par[ci] = i2
                    P, PK = Pt[i2], f"Pt{i2}"
                    sl_ = slice(c0, c0 + 128)
                    if d == 0:
                        S.dve(lambda e: e.tensor_tensor_scan(out=P[:], data0=onesf[:], data1=lgf[:, sl_], initial=0.0,
                                                             op0=ALU.mult, op1=ALU.add), ["lgf", "onesf"], [PK])
                        lastc = 127
                    else:
                        S.dve(lambda e: e.tensor_tensor_scan(out=Gt[:], data0=onesf[:], data1=lgf[:, sl_], initial=0.0,
                                                             op0=ALU.mult, op1=ALU.add), ["lgf", "onesf"], ["Gt"])
                        S.pool(lambda e: e.tensor_tensor(out=P[:], in0=lgf[:, sl_], in1=Gt[:], op=ALU.subtract), ["lgf", "Gt"], [PK])
                        S.dve(lambda e: e.tensor_scalar(out=P[:], in0=P[:], scalar1=Gt[:, 127:128], scalar2=None, op0=ALU.add),
                              [PK, "Gt"], [PK])
                        lastc = 0
                    R, RK = rsm[i2], f"rsm{i2}"
                    S.dve(lambda e: e.tensor_scalar(out=R[:, 0:1], in0=P[:, 63:64], scalar1=-1.0, scalar2=None, op0=ALU.mult), [PK], [RK])
                    S.act(lambda e: e.activation(out=E[i2][0][:], in_=P[:], func=AF.Exp, bias=R[:, 0:1]), [PK, RK], [f"E{i2}0"])
                    S.act(lambda e: e.activation(out=E[i2][1][:], in_=P[:], func=AF.Exp, scale=-1.0, bias=P[:, 63:64]), [PK], [f"E{i2}1"])
                    S.act(lambda e: e.activation(out=E[i2][2][:], in_=P[:], func=AF.Exp), [PK], [f"E{i2}2"])
                    S.act(lambda e: e.activation(out=R[:, 1:2], in_=P[:, lastc:lastc + 1], func=AF.Exp), [PK, RK], [RK])
                    S.act(lambda e: e.activation(out=R[:, 2:3], in_=P[:, lastc:lastc + 1], func=AF.Exp, bias=R[:, 0:1]), [PK, RK], [RK])
                    S.pool(lambda e: e.tensor_tensor(out=qg[i2][:], in0=rq[:, sl_], in1=E[i2][0][:], op=ALU.mult), ["rq", f"E{i2}0"], [f"qg{i2}"])
                    S.dve(lambda e: e.tensor_tensor(out=kg[i2][:], in0=kkt[:, sl_], in1=E[i2][1][:], op=ALU.mult), ["kkt", f"E{i2}1"], [f"kg{i2}"])
                    S.pool(lambda e: e.tensor_tensor(out=qs[i2][:], in0=rq[:, sl_], in1=E[i2][2][:], op=ALU.mult), ["rq", f"E{i2}2"], [f"qs{i2}"])
                    S.pe(lambda e: e.matmul(pA[i2][:], lhsT=kg[i2][:], rhs=qg[i2][:], start=True, stop=True),
                         [f"kg{i2}", f"qg{i2}"], [f"pA{i2}"])
                    S.dve(lambda e: e.tensor_scalar(out=Acl[i2][:], in0=pA[i2][:], scalar1=1e30, scalar2=-1e30, op0=ALU.min, op1=ALU.max),
                          [f"pA{i2}"], [f"Acl{i2}"])
                    S.pool(lambda e: e.tensor_tensor(out=ATs[i2][:], in0=Acl[i2][:], in1=mask[:], op=ALU.mult),
                           [f"Acl{i2}", "maskf", "maskb"], [f"AT{i2}"])
                    S.pe(lambda e: e.transpose(out=pT[i2][:], in_=kg[i2][:], identity=identb[:]), [f"kg{i2}"], [f"pT{i2}"])
                    S.act(lambda e: e.copy(out=kgt[i2][:], in_=pT[i2][:]), [f"pT{i2}"], [f"kgt{i2}"])

                def chain(ci):
                    nonlocal ocount
                    n = order[ci]
                    c0 = n * 128
                    i2 = par[ci]
                    sl_ = slice(c0, c0 + 128)
                    R, RK = rsm[i2], f"rsm{i2}"
                    sbi = (ci + 1) % 2
                    S.pe(lambda e: e.matmul(pO[i2][:], lhsT=ri[:, n, :], rhs=ATs[i2][:], start=True, stop=(ci == 0)),
                         ["ri", f"AT{i2}"], [f"pO{i2}"], inc=(ci == 0))
                    if ci > 0:
                        S.pe(lambda e: e.matmul(pO[i2][:], lhsT=Sb[ci % 2][:], rhs=qs[i2][:], start=False, stop=True),
                             [f"Sb{ci % 2}", f"qs{i2}"], [f"pO{i2}"])
                    S.pe(lambda e: e.matmul(pW[:], lhsT=kgt[i2][:], rhs=ri[:, n, :], start=True, stop=True),
                         [f"kgt{i2}", "ri"], ["pW"])
                    if ci == 0:
                        S.dve(lambda e: e.tensor_scalar(out=Sf[:], in0=pW[:], scalar1=R[:, 2:3], scalar2=None, op0=ALU.mult),
                              ["pW", RK], ["Sf"])
                    else:
                        S.dve(lambda e: e.tensor_scalar(out=wtmp[:], in0=pW[:], scalar1=R[:, 2:3], scalar2=None, op0=ALU.mult),
                              ["pW", RK], ["wtmp"])
                        S.dve(lambda e: e.scalar_tensor_tensor(out=Sf[:], in0=Sf[:], scalar=R[:, 1:2], in1=wtmp[:],
                                                               op0=ALU.mult, op1=ALU.add), ["Sf", "wtmp", RK], ["Sf"])
                    S.pool(lambda e: e.tensor_copy(out=Sb[sbi][:], in_=Sf[:]), ["Sf"], [f"Sb{sbi}"])
                    if d == 0:
                        S.act(lambda e: e.copy(out=ofw[:, sl_], in_=pO[i2][:]), [f"pO{i2}"], ["ofw"])
                    else:
                        g4 = n % 4
                        S.dve(lambda e: e.tensor_tensor(out=osum[:, g4 * 128:(g4 + 1) * 128], in0=pO[i2][:], in1=ofw[:, sl_], op=ALU.add),
                              [f"pO{i2}", "ofw"], ["osum"])
                        if g4 == 0:
                            b0 = n * 128
                            S.act(lambda e: e.activation(out=osq[:], in_=osum[:], func=AF.Square), ["osum"], ["osq"])
                            S.pe(lambda e: e.matmul(pN[:], lhsT=onesb[:], rhs=osq[:], start=True, stop=True), ["osq", "onesb"], ["pN"])
                            rstd_from(sc, pN[:], 128, ors[:], ot1[:], ["pN"], ["ot1", "ors"])
                            S.dve(lambda e: e.scalar_tensor_tensor(out=ot1[:], in0=osum[:], scalar=ronw[:, 0:1], in1=ors[:],
                                                                   op0=ALU.mult, op1=ALU.mult), ["osum", "ors", "ronw", "ot1"], ["ot1"])
                            oi = ocount % 2
                            ocount += 1
                            S.pool(lambda e: e.tensor_tensor(out=omo[oi][:], in0=ot1[:], in1=rg[:, b0:b0 + 512], op=ALU.mult),
                                   ["ot1", "rg"], [f"omo{oi}"])
                            S.st(MIXT[s, 4 + h, :, b0:b0 + 512], omo[oi][:], r=[f"omo{oi}"])

                prep(0)
                for ci in range(NB):
                    if ci + 1 < NB:
                        prep(ci + 1)
                    chain(ci)
    sc.close()

    if upto == 3:
        return nc
    W = FW
    NBW = W // 128
    sc = Scope()
    wo = sc.sb("wo", [128, 8, D], BF16)
    wu = sc.sb("wu", [128, 8, 2 * DFF], BF16)
    for c in range(8):
        S.ld(wo[:, c, :], WOUT[:, c, :], w=["wo"])
        S.ld(wu[:, c, :], WUP[:, c, :], w=["wu"])
    wd = [sc.sb(f"wd{i}", [128, 22, 128], BF16) for i in range(2)]
    xq = [sc.sb(f"xq{i}", [128, NBW, 128], F32) for i in range(4)]
    x1T = sc.sb("x1T", [128, 8, W], F32)
    fsq = [sc.sb(f"fsq{i}", [128, W], BF16) for i in range(2)]
    ft1 = sc.sb("ft1", [128, W], F32)
    frs = sc.sb("frs", [128, W], F32)
    h2T = sc.sb("h2T", [128, 8, W], BF16)
    mx = h2T
    big = sc.sb("big", [128, 8, W], F32)
    sg = [sc.sb(f"sg{i}", [128, W], F32) for i in range(2)]
    gT = sc.sb("gT", [128, 22, W], BF16)
    yo = sc.sb("yo", [128, D], F32)
    Bk = [sc.ps(f"B{i}", [128, 512]) for i in range(6)]
    pY = sc.ps("pY", [128, 2, 512])
    bank = [(Bk[i][:, 0:W], f"B{i}") for i in range(6)] + [(pY[:, 0, 0:W], "B6"), (pY[:, 1, 0:W], "B7")]
    p1 = [bank[0][0], bank[1][0]]
    pS = bank[2][0]
    pui = 0
    for i in range(4):
        S.pool(lambda e: e.memset(xq[i][:], 0.0), [], [f"xq{i}"])
    S.pool(lambda e: e.memset(mx[:], 0.0), [], ["h2T"])
    FTt = W - 2
    starts = list(range(0, L - FTt, FTt)) + [L - FTt]
    ui = 0
    wdi = 0
    xqi = 0
    for s in range(NSEQ):
        for t0 in starts:
            w0 = t0 - 1
            lo, hi = max(w0, 0), min(w0 + W, L)
            if P4SUB < 2:
                continue
            S.ld(mx[:, :, lo - w0:hi - w0], MIXT[s, :, :, lo:hi].rearrange("c p t -> p c t"), w=["h2T"])
            for c in range(8):
                pi = c % 2
                xi = xqi % 4
                xqi += 1
                for b in range(NBW):
                    r0, r1 = max(w0 + b * 128, lo), min(w0 + (b + 1) * 128, hi)
                    S.ld(xq[xi][r0 - (w0 + b * 128):r1 - (w0 + b * 128), b, :], x[s, r0:r1, c * 128:(c + 1) * 128], w=[f"xq{xi}"])
                if P4SUB < 3:
                    continue
                for k in range(8):
                    S.pe(lambda e: e.matmul(p1[pi][:], lhsT=wo[:, k, c * 128:(c + 1) * 128], rhs=mx[:, k, :],
                                            start=(k == 0), stop=False), ["wo", "h2T"], [f"B{pi}"], inc=False)
                for b in range(NBW):
                    S.pe(lambda e: e.matmul(p1[pi][:, b * 128:(b + 1) * 128], lhsT=xq[xi][:, b, :], rhs=identf[:],
                                            start=False, stop=(b == NBW - 1)), [f"xq{xi}", "identf"], [f"B{pi}"], inc=(b == NBW - 1))
                if P4SUB < 4:
                    continue
                S.dve(lambda e: e.tensor_copy(out=x1T[:, c, :], in_=p1[pi][:]), [f"B{pi}"], ["x1T"])
                if P4SUB == 4 and os.environ.get("P4V") == "copyonly":
                    continue
                S.act(lambda e: e.activation(out=fsq[pi][:], in_=x1T[:, c, :], func=AF.Square), ["x1T"], [f"fsq{pi}"])
                if P4SUB == 4 and os.environ.get("P4V") == "nosum":
                    continue
                S.pe(lambda e: e.matmul(pS[:], lhsT=onesb[:], rhs=fsq[pi][:], start=(c == 0), stop=(c == 7)),
                     [f"fsq{pi}", "onesb"], ["B2"], inc=(c == 7))
            if P4SUB < 5:
                continue
            rstd_from(sc, pS[:], D, frs[:], ft1[:], ["B2"], ["ft1", "frs"])
            for c in range(8):
                eng = S.dve if c % 2 == 0 else S.pool
                eng(lambda e: e.tensor_tensor(out=h2T[:, c, :], in0=x1T[:, c, :], in1=frs[:], op=ALU.mult), ["x1T", "frs"], ["h2T"])
            if w0 < 0:
                S.dve(lambda e: e.memset(h2T[:, :, 0:1], 0.0), ["h2T"], ["h2T"])
            if w0 + W > L:
                S.dve(lambda e: e.memset(h2T[:, :, W - 1:W], 0.0), ["h2T"], ["h2T"])
            if P4STAGE < 1:
                continue
            a0, a1 = 1, W - 1
            for j in range(22):
                u2 = ui % 2
                ui += 1
                pis = []
                for k2, uc in enumerate([j, 22 + j]):
                    PU, PK = bank[pui % 8]
                    pui += 1
                    pis.append((PU, PK))
                    for k in range(8):
                        S.pe(lambda e: e.matmul(PU, lhsT=wu[:, k, uc * 128:(uc + 1) * 128], rhs=h2T[:, k, :],
                                                start=(k == 0), stop=(k == 7)), ["wu", "h2T"], [PK], inc=(k == 7))
                for k2, uc in enumerate([j, 22 + j]):
                    PU, PK = pis[k2]
                    cs = 4 + u2 * 2 + k2
                    S.act(lambda e: e.activation(out=big[:, cs, a0:a1], in_=PU[:, a0 - 1:a1 - 1], func=AF.Identity,
                                                 scale=cw[:, uc, 0:1], bias=cb[:, uc:uc + 1]), [PK, "cw", "cb"], [f"big{cs}"])
                for tap in (1, 2):
                    for k2, uc in enumerate([j, 22 + j]):
                        PU, PK = pis[k2]
                        cs = 4 + u2 * 2 + k2
                        S.dve(lambda e: e.scalar_tensor_tensor(out=big[:, cs, a0:a1], in0=PU[:, a0 - 1 + tap:a1 - 1 + tap],
                                                               scalar=cw[:, uc, tap:tap + 1], in1=big[:, cs, a0:a1],
                                                               op0=ALU.mult, op1=ALU.add), [PK, f"big{cs}", "cw"], [f"big{cs}"])
                cg, cu = 4 + u2 * 2, 4 + u2 * 2 + 1
                S.act(lambda e: e.activation(out=sg[u2][:, 1:W - 1], in_=big[:, cg, 1:W - 1], func=AF.Silu), [f"big{cg}"], [f"sg{u2}"])
                S.dve(lambda e: e.tensor_tensor(out=gT[:, j, 0:FTt], in0=sg[u2][:, 1:W - 1], in1=big[:, cu, 1:W - 1], op=ALU.mult),
                      [f"sg{u2}", f"big{cu}"], ["gT"])
            if P4STAGE < 2:
                continue
            for c in range(8):
                wi = wdi % 2
                wdi += 1
                S.ld(wd[wi][:], WDN[c, :, :, :], w=[f"wd{wi}"])
                PU, PK = bank[pui % 6]
                pui += 1
                for j in range(22):
                    S.pe(lambda e: e.matmul(PU[:, 0:FTt], lhsT=wd[wi][:, j, :], rhs=gT[:, j, 0:FTt], start=(j == 0), stop=(j == 21)),
                         [f"wd{wi}", "gT"], [PK], inc=(j == 21))
                S.dve(lambda e: e.tensor_tensor(out=big[:, c, 0:FTt], in0=PU[:, 0:FTt], in1=x1T[:, c, 1:W - 1], op=ALU.add),
                      [PK, "x1T"], [f"big{c}"])
            if P4STAGE < 3:
                continue
            for b in range(NBW):
                nb = 128 if b < NBW - 1 else FTt - 128 * (NBW - 1)
                for c in range(8):
                    S.pe(lambda e: e.transpose(out=pY[0:nb, c // 4, (c % 4) * 128:(c % 4 + 1) * 128], in_=big[:, c, b * 128:b * 128 + nb],
                                               identity=identf[:]), [f"big{c}", "identf"], [f"B{6 + c // 4}"], inc=(c % 4 == 3))
                S.act(lambda e: e.copy(out=yo[0:nb, 0:512], in_=pY[0:nb, 0, :]), ["B6"], ["yo"])
                S.dve(lambda e: e.tensor_copy(out=yo[0:nb, 512:1024], in_=pY[0:nb, 1, :]), ["B7"], ["yo"])
                S.st(y[s, t0 + b * 128:t0 + b * 128 + nb, :], yo[0:nb, :], r=["yo"])
    sc.close()
    return nc


def host_params(inp):
    f = np.float32
    g = lambda k: np.asarray(inp[k], dtype=f)
    P = {}
    P["w_in"] = np.ascontiguousarray(g("w_in")[0])
    P["w_out"] = np.ascontiguousarray(g("w_out")[0])
    P["w_up"] = np.ascontiguousarray(g("w_up")[0])
    P["w_down"] = np.ascontiguousarray(g("w_down")[0])
    P["p_nmw"] = np.ascontiguousarray(g("norm_mix_w")[0].reshape(8, 128).T)
    P["p_nfw"] = np.ascontiguousarray(g("norm_ffn_w")[0].reshape(8, 128).T)
    P["p_qnw"] = np.ascontiguousarray(np.tile(g("q_norm_w")[0], 2).reshape(128, 1))
    P["p_knw"] = np.ascontiguousarray(np.tile(g("k_norm_w")[0], 2).reshape(128, 1))
    lam = np.stack([g("lambda_q1")[0], g("lambda_k1")[0], g("lambda_q2")[0], g("lambda_k2")[0]])
    P["p_lam"] = np.ascontiguousarray(np.broadcast_to(lam[None], (128, 4, 64)))
    P["p_aonw"] = np.ascontiguousarray(g("attn_out_norm_w")[0].reshape(128, 1))
    P["p_ronw"] = np.ascontiguousarray(g("rec_out_norm_w")[0].reshape(128, 1))
    lb = np.stack([g("lb_fwd"), g("lb_bwd")])
    lb = lb.reshape(2, 2, NH, 128)
    P["p_lb"] = np.ascontiguousarray(lb.transpose(3, 0, 2, 1))
    P["p_cw"] = np.ascontiguousarray(g("conv_w")[0].reshape(3, 44, 128).transpose(2, 1, 0))
    P["p_cb"] = np.ascontiguousarray(g("conv_b")[0].reshape(44, 128).T)
    return P


_CACHE = {}


def kernel(**inputs):
    xp = np.asarray(inputs["x_prompt"], dtype=np.float32)
    xs = np.asarray(inputs["x_sample"], dtype=np.float32)
    L = xs.shape[1]
    ncores = 8
    NSEQ = 2
    P = host_params(inputs)
    P.update(host_consts(L))
    in_maps = []
    for c in range(ncores):
        m = dict(P)
        m["x"] = np.ascontiguousarray(np.stack([xs[c], xp[c % 2]]))
        in_maps.append(m)
    key = (L, NSEQ)
    if key not in _CACHE:
        _CACHE[key] = build(L, NSEQ)
    nc = _CACHE[key]
    res = run_bass_kernel_spmd(nc, in_maps, core_ids=list(range(ncores)))
    y_sample = np.stack([np.asarray(res.results[c]["y"][0], dtype=np.float32) for c in range(ncores)])
    y_prompt = np.stack([np.asarray(res.results[c]["y"][1], dtype=np.float32) for c in range(2)])
    return (y_prompt, y_sample)
```
